# Optimizing a Trainium2 kernel written in Bass

```python
import jax, jax.numpy as jnp
from jax import lax
import numpy as np

D_MODEL = 1024
BATCH = 16
SEQ = 2048
DEPTH = 4

GRID_W = 64
CTX_LEN = 256
N_MIXERS = 2
CONV_WIDTH = 3
GLA_HEADS = 4
GLA_DK_TOT = D_MODEL // 2
GLA_DV_TOT = D_MODEL
GLA_DK = GLA_DK_TOT // GLA_HEADS
GLA_DV = GLA_DV_TOT // GLA_HEADS
GLA_GATE_RANK = 16
GLA_TAU = 16.0
GLA_CHUNK = 64
D_FF = -(-8 * D_MODEL // (3 * 256)) * 256
N_ADA = 6
EPS = 1e-6

kernel_name = "hybrid_conv_gla_diffusion_trunk"


def rmsnorm(x, g):
    xf = x.astype(jnp.float32)
    y = xf * lax.rsqrt(jnp.mean(xf * xf, axis=-1, keepdims=True) + EPS)
    return (y * g.astype(jnp.float32)).astype(x.dtype)


def ada_terms(cond, w, b):
    return jnp.split(jax.nn.silu(cond) @ w + b, N_ADA, axis=-1)


def modulate(h, shift, scale):
    return h * (1.0 + scale) + shift


def conv3_seq(u, w):
    up = jnp.pad(u, ((0, 0), (1, 1), (0, 0)))
    return w[0] * up[:, :-2] + w[1] * up[:, 1:-1] + w[2] * up[:, 2:]


def conv3_grid(u, w):
    B, L, C = u.shape
    rows = L // GRID_W
    return conv3_seq(u.reshape(B * rows, GRID_W, C), w).reshape(B, L, C)


def short_conv_mixer(h, w_in, conv_w, w_out, on_grid):
    gate_b, gate_c, hv = jnp.split(h @ w_in, 3, axis=-1)
    u = gate_c * hv
    conv = conv3_grid(u, conv_w) if on_grid else conv3_seq(u, conv_w)
    return (gate_b * conv) @ w_out


def gla_project(h, w_in, b_r, w_a1, w_a2, b_a):
    B, L, _ = h.shape
    q, k, v, r = jnp.split(h @ w_in, [GLA_DK_TOT, 2 * GLA_DK_TOT, 2 * GLA_DK_TOT + GLA_DV_TOT], axis=-1)

    def heads(t, d):
        return t.reshape(B, L, GLA_HEADS, d).transpose(0, 2, 1, 3).astype(jnp.float32)

    q = heads(q, GLA_DK) * (GLA_DK ** -0.5)
    k = heads(k, GLA_DK)
    v = heads(v, GLA_DV)
    logit = jnp.einsum('zblr,zrk->zblk', jnp.einsum('bld,zdr->zblr', h, w_a1), w_a2) + b_a[:, None, None, :]
    la = jax.nn.log_sigmoid(logit.astype(jnp.float32)) / GLA_TAU
    la = la.reshape(2, B, L, GLA_HEADS, GLA_DK).transpose(0, 1, 3, 2, 4)
    return q, k, v, r + b_r, la


def gla_chunked(q, k, v, la, s0, with_output):
    B, H, T, dk = q.shape
    dv = v.shape[-1]
    n = T // GLA_CHUNK
    q = q.reshape(B, H, n, GLA_CHUNK, dk)
    k = k.reshape(B, H, n, GLA_CHUNK, dk)
    v = v.reshape(B, H, n, GLA_CHUNK, dv)
    b = jnp.cumsum(la.reshape(B, H, n, GLA_CHUNK, dk), axis=3)
    b_last = b[:, :, :, -1]
    k_dec = k * jnp.exp(b_last[:, :, :, None] - b)
    u = jnp.einsum('bhnsd,bhnse->bhnde', k_dec, v)

    def step(state, inp):
        decay, u_n = inp
        return decay[..., None] * state + u_n, state

    s_final, s_prev = lax.scan(step, s0, (jnp.moveaxis(jnp.exp(b_last), 2, 0), jnp.moveaxis(u, 2, 0)))
    if not with_output:
        return None, s_final
    s_prev = jnp.moveaxis(s_prev, 0, 2)
    q_dec = q * jnp.exp(b)
    o_inter = jnp.einsum('bhncd,bhnde->bhnce', q_dec, s_prev)
    scores = jnp.einsum('bhncd,bhnsd->bhncs', q_dec, k * jnp.exp(-b))
    mask = jnp.tril(jnp.ones((GLA_CHUNK, GLA_CHUNK), dtype=bool))
    scores = jnp.where(mask, scores, 0.0)
    o_intra = jnp.einsum('bhncs,bhnse->bhnce', scores, v)
    return (o_inter + o_intra).reshape(B, H, T, dv), s_final


def gla_output(o, r, norm_g, w_out, dtype):
    B, H, T, dv = o.shape
    o = rmsnorm(o.transpose(0, 2, 1, 3), norm_g).reshape(B, T, H * dv).astype(dtype)
    return (o * jax.nn.silu(r)) @ w_out


def gla_mixer(hl, hc, w_in, b_r, w_a1, w_a2, b_a, norm_g, w_out, ctx_out):
    ql, kl, vl, rl, la_l = gla_project(hl, w_in, b_r, w_a1, w_a2, b_a)
    qc, kc, vc, rc, la_c = gla_project(hc, w_in, b_r, w_a1, w_a2, b_a)
    flip = lambda t: jnp.flip(t, axis=2)
    s0 = jnp.zeros((hl.shape[0], GLA_HEADS, GLA_DK, GLA_DV), jnp.float32)
    o_cf, s_cf = gla_chunked(qc, kc, vc, la_c[0], s0, ctx_out)
    o_cb, s_cb = gla_chunked(flip(qc), flip(kc), flip(vc), flip(la_c[1]), s0, ctx_out)
    o_lf, _ = gla_chunked(ql, kl, vl, la_l[0], s_cf, True)
    o_lb, _ = gla_chunked(flip(ql), flip(kl), flip(vl), flip(la_l[1]), s_cb, True)
    yl = gla_output(o_lf + flip(o_lb), rl, norm_g, w_out, hl.dtype)
    yc = gla_output(o_cf + flip(o_cb), rc, norm_g, w_out, hc.dtype) if ctx_out else None
    return yl, yc


def swiglu(h, w_in, w_out):
    gate, up = jnp.split(h @ w_in, 2, axis=-1)
    return (jax.nn.silu(gate) * up) @ w_out


def setup_inputs(seed: int = 0) -> dict:
    key = jax.random.key(seed)
    ks = jax.random.split(key, 24)
    n_conv = (DEPTH + N_MIXERS - 1) // N_MIXERS
    n_gla = DEPTH // N_MIXERS
    nrm = lambda k, shape, s: jax.random.normal(k, shape, jnp.float32) * s
    d = D_MODEL
    return {
        "x": nrm(ks[0], (BATCH, SEQ, d), 1.0),
        "c": nrm(ks[1], (BATCH, d), 1.0),
        "ctx": nrm(ks[2], (BATCH, CTX_LEN, d), 1.0),
        "c_ctx": nrm(ks[3], (d,), 1.0),
        "ada_w": nrm(ks[4], (DEPTH, d, N_ADA * d), 0.5 * d ** -0.5),
        "ada_b": nrm(ks[5], (DEPTH, N_ADA * d), 0.01),
        "norm1_g": 1.0 + nrm(ks[6], (DEPTH, d), 0.02),
        "norm2_g": 1.0 + nrm(ks[7], (DEPTH, d), 0.02),
        "conv_w_in": nrm(ks[8], (n_conv, d, 3 * d), d ** -0.5),
        "conv_w": nrm(ks[9], (n_conv, CONV_WIDTH, d), CONV_WIDTH ** -0.5),
        "conv_w_out": nrm(ks[10], (n_conv, d, d), d ** -0.5),
        "gla_w_in": nrm(ks[11], (n_gla, d, 2 * GLA_DK_TOT + 2 * GLA_DV_TOT), d ** -0.5),
        "gla_b_r": nrm(ks[12], (n_gla, GLA_DV_TOT), 0.01),
        "gla_w_a1": nrm(ks[13], (n_gla, 2, d, GLA_GATE_RANK), d ** -0.5),
        "gla_w_a2": nrm(ks[14], (n_gla, 2, GLA_GATE_RANK, GLA_DK_TOT), GLA_GATE_RANK ** -0.5),
        "gla_b_a": nrm(ks[15], (n_gla, 2, GLA_DK_TOT), 0.01),
        "gla_norm_g": 1.0 + nrm(ks[16], (n_gla, GLA_DV), 0.02),
        "gla_w_out": nrm(ks[17], (n_gla, GLA_DV_TOT, d), GLA_DV_TOT ** -0.5),
        "ffn_w_in": nrm(ks[18], (DEPTH, d, 2 * D_FF), d ** -0.5),
        "ffn_w_out": nrm(ks[19], (DEPTH, D_FF, d), D_FF ** -0.5),
        "final_g": 1.0 + nrm(ks[20], (d,), 0.02),
    }


def reference(x, c, ctx, c_ctx, ada_w, ada_b, norm1_g, norm2_g, conv_w_in, conv_w, conv_w_out,
              gla_w_in, gla_b_r, gla_w_a1, gla_w_a2, gla_b_a, gla_norm_g, gla_w_out,
              ffn_w_in, ffn_w_out, final_g):
    cond_lat = c[:, None, :]
    for i in range(DEPTH):
        last = i == DEPTH - 1
        kind = i % N_MIXERS
        j = i // N_MIXERS
        need_ctx_in = (not last) or kind == 1
        sh1, sc1, g1, sh2, sc2, g2 = ada_terms(cond_lat, ada_w[i], ada_b[i])
        hl = modulate(rmsnorm(x, norm1_g[i]), sh1, sc1)
        if need_ctx_in:
            csh1, csc1, cg1, csh2, csc2, cg2 = ada_terms(c_ctx, ada_w[i], ada_b[i])
            hc = modulate(rmsnorm(ctx, norm1_g[i]), csh1, csc1)
        if kind == 0:
            yl = short_conv_mixer(hl, conv_w_in[j], conv_w[j], conv_w_out[j], True)
            yc = short_conv_mixer(hc, conv_w_in[j], conv_w[j], conv_w_out[j], False) if not last else None
        else:
            yl, yc = gla_mixer(hl, hc, gla_w_in[j], gla_b_r[j], gla_w_a1[j], gla_w_a2[j], gla_b_a[j],
                               gla_norm_g[j], gla_w_out[j], not last)
        x = x + g1 * yl
        x = x + g2 * swiglu(modulate(rmsnorm(x, norm2_g[i]), sh2, sc2), ffn_w_in[i], ffn_w_out[i])
        if not last:
            ctx = ctx + cg1 * yc
            ctx = ctx + cg2 * swiglu(modulate(rmsnorm(ctx, norm2_g[i]), csh2, csc2), ffn_w_in[i], ffn_w_out[i])
    return rmsnorm(x, final_g)
```

```python
import numpy as np
from contextlib import ExitStack
import concourse.bass as bass
import concourse.mybir as mybir
from concourse.bass_utils import run_bass_kernel_spmd

F32 = mybir.dt.float32
BF16 = mybir.dt.bfloat16
AF = mybir.ActivationFunctionType
ALU = mybir.AluOpType

D = 1024
KC = 8
TCX = 256
TL = 2048
T = TCX + TL
NCH = T // 128
DFF = 2816
NF = DFF // 128
DEPTH = 4
EPS = 1e-6
NCORES = 8
CORES_PER_LAUNCH = 4
GRP = 6

ALL_TILES = [(0, 256)] + [(256 + 512 * i, 512) for i in range(4)]
LAT_TILES = ALL_TILES[1:]


class Buf:
    _reg = []

    def __init__(self, name, region=None):
        self.name = name
        self.w = None
        self.r = {}
        self.ov = []
        self.region = region
        self.dsem = None
        if region is not None:
            for o in Buf._reg:
                if o.region[0] == region[0] and o.region[1] < region[2] and region[1] < o.region[2]:
                    o.ov.append(self)
                    self.ov.append(o)
            Buf._reg.append(self)


class Q:
    def __init__(self, name, eng, sem):
        self.name, self.e, self.sem, self.cnt, self.waited = name, eng, sem, 0, {}


class Prog:
    def __init__(self, nc, es):
        self.nc = nc
        self.es = es
        self.nsem = 0
        mk = lambda n, e: Q(n, e, self.sem(n))
        self.pe = mk("pe", nc.tensor)
        self.act = mk("act", nc.scalar)
        self.dve = mk("dve", nc.vector)
        self.pool = mk("pool", nc.gpsimd)
        self.sp = mk("sp", nc.sync)

    def sem(self, name):
        self.nsem += 1
        return self.es.enter_context(self.nc.semaphore(f"s_{name}_{self.nsem}"))

    def sb(self, name, shape, dt):
        return self.es.enter_context(self.nc.sbuf_tensor("t_" + name, shape, dt))

    def _deps(self, reads, writes):
        d = {}

        def add(t):
            if t is not None:
                if d.get(t[0], 0) < t[1]:
                    d[t[0]] = t[1]

        for b in reads:
            add(b.w)
            for o in b.ov:
                add(o.w)
        for b in writes:
            for bb in [b] + b.ov:
                add(bb.w)
                for s, v in bb.r.items():
                    add((s, v))
        return d

    def _wait(self, q, d):
        for s, v in d.items():
            if q.name == "pe" and s is q.sem:
                continue
            if q.waited.get(s, 0) < v:
                q.e.wait_ge(s, v)
                q.waited[s] = v

    def op(self, q, fn, reads=(), writes=()):
        self._wait(q, self._deps(reads, writes))
        ins = fn()
        q.cnt += 1
        ins.then_inc(q.sem, 1)
        tag = (q.sem, q.cnt)
        for b in writes:
            b.w = tag
            b.r = {}
        for b in reads:
            b.r[q.sem] = q.cnt

    def dma(self, q, pairs, reads=(), writes=(), sembuf=None):
        sb = sembuf if sembuf is not None else (writes[0] if writes else reads[0])
        if sb.dsem is None:
            sb.dsem = [self.sem("d" + sb.name), 0]
        self._wait(q, self._deps(reads, writes))
        for o, i in pairs:
            q.e.dma_start(out=o, in_=i).then_inc(sb.dsem[0], 16)
            sb.dsem[1] += 16
        tag = (sb.dsem[0], sb.dsem[1])
        for b in writes:
            b.w = tag
            b.r = {}
        for b in reads:
            b.r[tag[0]] = tag[1]
        return tag


def build_program(layers=(0, 1, 2, 3), first=True, final=True, phases=("ada", "norm1", "mixer", "norm2", "ffn")):
    Buf._reg = []
    nc = bass.Bass("TRN2", target_bir_lowering=False)

    def din(name, shape):
        return nc.dram_tensor(name, list(shape), F32, kind="ExternalInput").ap()

    x_d = din("x", (2, TL, D))
    ctx_d = din("ctx", (2, TCX, D))
    c3_d = din("c3", (128, KC, 3))
    ada_w_d = din("ada_w", (DEPTH, D, 6 * D))
    ada_b_d = din("ada_b", (128, DEPTH, 48))
    n1g_d = din("n1g", (128, DEPTH, KC))
    n2g_d = din("n2g", (128, DEPTH, KC))
    conv_w_in_d = din("conv_w_in", (2, D, 3 * D))
    convw_d = din("convw", (128, 2, 3, KC))
    conv_w_out_d = din("conv_w_out", (2, D, D))
    gla_w_in_d = din("gla_w_in", (2, D, 3 * D))
    gla_br_d = din("gla_br", (2, 128, D))
    gla_wa1_d = din("gla_wa1", (2, 2, D, 16))
    gla_wa2_d = din("gla_wa2", (2, 2, 16, 512))
    gla_ba_d = din("gla_ba", (2, 2, 512))
    gla_ng_d = din("gla_ng", (2, 128, 256))
    gla_w_out_d = din("gla_w_out", (2, D, D))
    ffn_w_in_d = din("ffn_w_in", (DEPTH, D, 2 * DFF))
    ffn_w_out_d = din("ffn_w_out", (DEPTH, DFF, D))
    fing_d = din("fing", (128, KC))
    cst_d = din("cst", (128, 8, 128))
    xo_d = nc.dram_tensor("xo", [2, TL, D], F32, kind="ExternalOutput").ap()
    want_ctx_out = not final
    if want_ctx_out:
        ctxo_d = nc.dram_tensor("ctxo", [2, TCX, D], F32, kind="ExternalOutput").ap()

    with ExitStack() as es:
        P = Prog(nc, es)
        pe, act, dve, pool, sp = P.pe, P.act, P.dve, P.pool, P.sp

        xs = P.sb("xs", [128, KC, T], F32)
        hT = P.sb("hT", [128, KC, T], BF16)
        zflat = P.sb("zb", [128, GRP * T], BF16)
        zb = zflat[:, :].rearrange("p (k t) -> p k t", t=T)
        Wt = [P.sb(f"W{i}", [128, 6144], BF16) for i in range(3)]
        stage = [P.sb(f"stage{i}", [128, 512], F32) for i in range(2)]
        tmpF = [P.sb(f"tmpF{i}", [128, 512], F32) for i in range(3)]
        sqb = [P.sb(f"sq{i}", [128, 512], BF16) for i in range(2)]
        rs = P.sb("rs", [128, 512], F32)
        cst = P.sb("cst", [128, 8, 128], F32)
        ident_b = P.sb("ident_b", [128, 128], BF16)
        ones_b = P.sb("ones_b", [128, 128], BF16)
        ada_b_t = P.sb("ada_b_t", [128, DEPTH, 48], F32)
        ada_sb = P.sb("ada_sb", [128, DEPTH * 3 * 48], F32)
        Acoef = P.sb("Acoef", [128, DEPTH * 3 * 2 * KC], F32)
        n1g = P.sb("n1g", [128, DEPTH, KC], F32)
        n2g = P.sb("n2g", [128, DEPTH, KC], F32)
        convw = P.sb("convw", [128, 2 * 3 * KC], F32)
        fing = P.sb("fing", [128, KC], F32)
        c3f = P.sb("c3f", [128, KC, 3], F32)
        c3b = P.sb("c3b", [128, KC, 3], BF16)
        brh = P.sb("brh", [128, 256], F32)
        gng = P.sb("gng", [128, 256], F32)
        wa1 = P.sb("wa1", [128, KC, 32], BF16)
        wa2 = P.sb("wa2", [17, 2, 512], BF16)
        ec_t = P.sb("ec_t", [128, 128], F32)
        Eq_t = P.sb("Eq_t", [128, 256], F32)
        Ek_t = P.sb("Ek_t", [128, 256], F32)
        kdec_t = [P.sb(f"kdec{i}", [128, 128], BF16) for i in range(2)]
        qdec_t = P.sb("qdec", [128, 2, 128], BF16)
        kinv_t = P.sb("kinv", [128, 2, 128], BF16)
        sm_t = P.sb("sm", [128, 256], BF16)
        Sst = P.sb("Sst", [128, 256], F32)
        Sbf = [P.sb(f"Sbf{i}", [128, 256], BF16) for i in range(2)]
        dec_t = P.sb("dec_t", [128, 4], F32)
        og_t = P.sb("og_t", [128, 256], BF16)
        ogT_t = P.sb("ogT_t", [128, 2, 128], BF16)

        NPS = 7
        psf = [es.enter_context(nc.psum_tensor(f"ps{i}", [128, 512], F32)) for i in range(NPS)]
        ptb = es.enter_context(nc.psum_tensor("ptb", [128, 1024], BF16))

        B_xs = [[Buf(f"xs{k}_{n}") for n in range(NCH)] for k in range(KC)]
        B_hT = [Buf(f"hT{n}") for n in range(NCH)]
        B_z = [Buf(f"z{i}", ("z", i * T, (i + 1) * T)) for i in range(GRP)]
        B_W = [Buf(f"W{i}") for i in range(3)]
        B_stage = [Buf(f"stage{i}") for i in range(2)]
        B_tmp = [Buf(f"tmpF{i}") for i in range(3)]
        B_sq = [Buf(f"sq{i}") for i in range(2)]
        B_rs = Buf("rs")
        B_ps = [Buf(f"ps{i}") for i in range(NPS)]
        B_ptb = Buf("ptb")
        B_cst = Buf("cst")
        B_par = Buf("params")
        B_ada = Buf("ada")
        B_c3b = Buf("c3b")
        B_gl = Buf("glasmall")
        B_wa = Buf("wa")
        B_ec, B_Eq, B_Ek = Buf("ec"), Buf("Eq"), Buf("Ek")
        B_kdec = [Buf("kdec0"), Buf("kdec1")]
        B_qdec, B_kinv, B_sm = Buf("qdec"), Buf("kinv"), Buf("sm")
        B_S = Buf("Sst")
        B_Sbf = [Buf("Sbf0"), Buf("Sbf1")]
        B_dec = Buf("dec")
        B_og, B_ogT = Buf("og"), Buf("ogT")
        SB0, V0, R1F0, R1B0 = 0, 4608, 9216, 11520
        B_sbp = Buf("sbprev", ("z", SB0, V0))
        B_vtm = Buf("vtm", ("z", V0, R1F0))
        B_r1 = [Buf("r1f", ("z", R1F0, R1B0)), Buf("r1b", ("z", R1B0, R1B0 + T))]
        sbprev = zflat[:, SB0:V0].rearrange("p (n e) -> p n e", e=256)
        vtm = zflat[:, V0:R1F0].rearrange("p (n e) -> p n e", e=256)
        r1aug = [zflat[0:17, R1F0:R1B0], zflat[0:17, R1B0:R1B0 + T]]

        cnt = {"ps": 0, "tmp": 0, "sq": 0, "w": 0, "st": 0}

        def next_ps():
            i = cnt["ps"] % NPS
            cnt["ps"] += 1
            return psf[i], B_ps[i]

        def next_tmp():
            i = cnt["tmp"] % 3
            cnt["tmp"] += 1
            return tmpF[i], B_tmp[i]

        def next_sq():
            i = cnt["sq"] % 2
            cnt["sq"] += 1
            return sqb[i], B_sq[i]

        def next_w():
            i = cnt["w"] % 3
            cnt["w"] += 1
            return Wt[i], B_W[i]

        def next_stage():
            i = cnt["st"] % 2
            cnt["st"] += 1
            return stage[i], B_stage[i]

        def chunks_of(c0, n):
            return range(c0 // 128, (c0 + n) // 128)

        def xs_bufs(k, c0, n):
            return [B_xs[k][c] for c in chunks_of(c0, n)]

        def hT_bufs(c0, n):
            return [B_hT[c] for c in chunks_of(c0, n)]

        def mm_group(out_ap, pairs, reads, writes):
            def fn():
                ins = None
                for i, (l, r) in enumerate(pairs):
                    ins = nc.tensor.matmul(out_ap, l, r, start=(i == 0), stop=(i == len(pairs) - 1))
                return ins
            P.op(pe, fn, reads=reads, writes=writes)

        P.dma(sp, [(cst[:], cst_d[:, :, :])], writes=[B_cst])
        P.dma(sp, [(ada_b_t[:], ada_b_d[:, :, :]), (n1g[:], n1g_d[:, :, :]), (n2g[:], n2g_d[:, :, :]),
                   (convw[:], convw_d[:, :, :, :].rearrange("p j t k -> p (j t k)")), (fing[:], fing_d[:, :]), (c3f[:], c3_d[:, :, :])], writes=[B_par])
        ident_f = cst[:, 0, :]
        M1, M2, M3, M4 = cst[:, 1, :], cst[:, 2, :], cst[:, 3, :], cst[:, 4, :]
        mask2 = cst[:, 5:7, :]
        negcol = cst[:, 7, 0:1]
        B_idb = Buf("identb")
        P.op(dve, lambda: nc.vector.tensor_copy(ident_b[:], cst[:, 0, :]), reads=[B_cst], writes=[B_idb])
        P.op(dve, lambda: nc.vector.memset(ones_b[:], 1.0), writes=[B_idb])
        P.op(act, lambda: nc.scalar.activation(out=c3b[:], in_=c3f[:], func=AF.Silu), reads=[B_par], writes=[B_c3b])

        def ada_steps():
            steps = []
            holder = {}
            for l in layers:
                for blk in range(8):
                    def load(w, Bw, l=l, blk=blk):
                        wv = w[:, 0:6144].rearrange("p (k n) -> p k n", n=768)
                        P.dma(pool, [(wv, ada_w_d[l, :, blk * 768:(blk + 1) * 768].rearrange("(k p) n -> p k n", p=128))],
                              writes=[Bw])

                    def comp(w, Bw, l=l, blk=blk):
                        wv = w[:, 0:6144].rearrange("p (k n) -> p k n", n=768)
                        if blk == 0:
                            holder["ps"] = next_ps()
                        psA, BpsA = holder["ps"]
                        for jj in range(6):
                            j = blk * 6 + jj
                            mm_group(psA[:, j * 3:(j + 1) * 3],
                                     [(wv[:, k, jj * 128:(jj + 1) * 128], c3b[:, k, :]) for k in range(KC)],
                                     reads=[Bw, B_c3b], writes=[BpsA])
                        if blk < 7:
                            return
                        psv = psA[:, 0:144].rearrange("p (j r) -> p j r", r=3)
                        for r in range(3):
                            P.op(dve, lambda r=r: nc.vector.tensor_tensor(ada_sb[:, (l * 3 + r) * 48:(l * 3 + r + 1) * 48], psv[:, :, r], ada_b_t[:, l, :], ALU.add),
                                 reads=[BpsA, B_par], writes=[B_ada])
                        for r in range(3):
                            for which, ng in ((0, n1g), (1, n2g)):
                                sc = ada_sb[:, (l * 3 + r) * 48 + 8 + 24 * which:(l * 3 + r) * 48 + 16 + 24 * which]
                                P.op(dve, lambda sc=sc, ng=ng, r=r, which=which: nc.vector.scalar_tensor_tensor(
                                    out=Acoef[:, ((l * 3 + r) * 2 + which) * KC:((l * 3 + r) * 2 + which + 1) * KC], in0=sc, scalar=1.0, in1=ng[:, l, :], op0=ALU.add, op1=ALU.mult),
                                    reads=[B_ada, B_par], writes=[B_ada])
                    steps.append((load, comp))
            return steps

        def ada_ap(l, r, idx, k):
            o = (l * 3 + r) * 48 + idx * 8 + k
            return ada_sb[:, o:o + 1]

        def load_x(b):
            for n in range(NCH):
                for half in range(2):
                    st, Bst = next_stage()
                    if n < 2:
                        src = ctx_d[b, n * 128:(n + 1) * 128, half * 512:(half + 1) * 512]
                    else:
                        src = x_d[b, (n - 2) * 128:(n - 1) * 128, half * 512:(half + 1) * 512]
                    P.dma(sp, [(st[:], src)], writes=[Bst])
                    ps, Bps = next_ps()

                    def fn(ps=ps, st=st):
                        ins = None
                        for q in range(4):
                            ins = nc.tensor.transpose(ps[:, q * 128:(q + 1) * 128], st[:, q * 128:(q + 1) * 128], ident_f)
                        return ins
                    P.op(pe, fn, reads=[Bst, B_cst], writes=[Bps])
                    dst = xs[:, half * 4:(half + 1) * 4, n * 128:(n + 1) * 128]
                    src_ps = ps[:, :].rearrange("p (q t) -> p q t", t=128)
                    wb = [B_xs[k][n] for k in range(half * 4, half * 4 + 4)]
                    if half == 0:
                        P.op(act, lambda dst=dst, s=src_ps: nc.scalar.copy(dst, s), reads=[Bps], writes=wb)
                    else:
                        P.op(dve, lambda dst=dst, s=src_ps: nc.vector.tensor_copy(dst, s), reads=[Bps], writes=wb)

        def store_x(b, chunks, dst_d, dst_off):
            for n in chunks:
                for half in range(2):
                    ps, Bps = next_ps()

                    def fn(ps=ps, n=n, half=half):
                        ins = None
                        for q in range(4):
                            ins = nc.tensor.transpose(ps[:, q * 128:(q + 1) * 128], xs[:, half * 4 + q, n * 128:(n + 1) * 128], ident_f)
                        return ins
                    P.op(pe, fn, reads=[B_xs[k][n] for k in range(half * 4, half * 4 + 4)] + [B_cst], writes=[Bps])
                    st, Bst = next_stage()
                    if half == 0:
                        P.op(act, lambda st=st, ps=ps: nc.scalar.copy(st[:], ps[:, :]), reads=[Bps], writes=[Bst])
                    else:
                        P.op(dve, lambda st=st, ps=ps: nc.vector.tensor_copy(st[:], ps[:, :]), reads=[Bps], writes=[Bst])
                    row = (n - dst_off) * 128
                    P.dma(sp, [(dst_d[b, row:row + 128, half * 512:(half + 1) * 512], st[:])], reads=[Bst], sembuf=Bst)

        def rstd_tile(c0, n):
            ps, Bps = next_ps()
            for k in range(KC):
                sq, Bsq = next_sq()
                xin = xs[:, k, c0:c0 + n]
                if k % 2 == 0:
                    P.op(act, lambda sq=sq, xin=xin: nc.scalar.activation(out=sq[:, :n], in_=xin, func=AF.Square),
                         reads=xs_bufs(k, c0, n), writes=[Bsq])
                else:
                    P.op(dve, lambda sq=sq, xin=xin: nc.vector.tensor_tensor(sq[:, :n], xin, xin, ALU.mult),
                         reads=xs_bufs(k, c0, n), writes=[Bsq])
                P.op(pe, lambda sq=sq, ps=ps, k=k: nc.tensor.matmul(ps[:, :n], ones_b[:], sq[:, :n], start=(k == 0), stop=(k == KC - 1)),
                     reads=[Bsq, B_idb], writes=[Bps])
            P.op(act, lambda ps=ps: nc.scalar.activation(out=rs[:, :n], in_=ps[:, :n], func=AF.Ln, bias=EPS, scale=1.0 / D),
                 reads=[Bps], writes=[B_rs])
            P.op(act, lambda: nc.scalar.activation(out=rs[:, :n], in_=rs[:, :n], func=AF.Exp, scale=-0.5),
                 reads=[B_rs], writes=[B_rs])

        def norm_phase(l, which, b, tiles):
            import os
            dbg = os.environ.get("NORMDBG", "")
            for (c0, n) in tiles:
                r = 2 if c0 < TCX else b
                rstd_tile(c0, n)
                if dbg == "rstd":
                    continue
                for k in range(KC):
                    t, Bt = next_tmp()
                    P.op(dve, lambda t=t, k=k: nc.vector.tensor_tensor(t[:, :n], xs[:, k, c0:c0 + n], rs[:, :n], ALU.mult),
                         reads=xs_bufs(k, c0, n) + [B_rs], writes=[Bt])
                    P.op(act, lambda t=t, k=k, r=r: nc.scalar.activation(
                        out=hT[:, k, c0:c0 + n], in_=t[:, :n], func=AF.Identity,
                        bias=ada_ap(l, r, 3 * which, k), scale=Acoef[:, ((l * 3 + r) * 2 + which) * KC + k:((l * 3 + r) * 2 + which) * KC + k + 1]),
                        reads=[Bt, B_ada], writes=hT_bufs(c0, n))

        def wout_compute(wv, Bw, G, tiles, l, b, gidx):
            for cp in range(KC):
                for (c0, n) in tiles:
                    r = 2 if c0 < TCX else b
                    ps, Bps = next_ps()
                    mm_group(ps[:, :n], [(wv[:, kk, cp * 128:(cp + 1) * 128], zb[:, kk, c0:c0 + n]) for kk in range(G)],
                             reads=[Bw] + B_z[:G], writes=[Bps])
                    xb = xs_bufs(cp, c0, n)
                    P.op(dve, lambda ps=ps, cp=cp, c0=c0, n=n, r=r: nc.vector.scalar_tensor_tensor(
                        out=xs[:, cp, c0:c0 + n], in0=ps[:, :n], scalar=ada_ap(l, r, gidx, cp), in1=xs[:, cp, c0:c0 + n],
                        op0=ALU.mult, op1=ALU.add), reads=[Bps, B_ada] + xb, writes=xb)

        def run_steps(steps, dist=2):
            slots = [None] * len(steps)

            def issue(i):
                if i < len(steps) and steps[i][0] is not None:
                    w, Bw = next_w()
                    steps[i][0](w, Bw)
                    slots[i] = (w, Bw)
            for i0 in range(dist):
                issue(i0)
            for i in range(len(steps)):
                issue(i + dist)
                w, Bw = slots[i] if slots[i] is not None else (None, None)
                steps[i][1](w, Bw)

        def conv_steps(l, j, b):
            tiles = ALL_TILES
            steps = []

            def mk_in(c):
                def load(w, Bw):
                    wv = w[:, 0:3072].rearrange("p (k n) -> p k n", n=384)
                    P.dma(pool, [(wv[:, :, s * 128:(s + 1) * 128],
                                  conv_w_in_d[j, :, s * 1024 + c * 128:s * 1024 + (c + 1) * 128].rearrange("(k p) n -> p k n", p=128))
                                 for s in range(3)], writes=[Bw])

                def comp(w, Bw):
                    wv = w[:, 0:3072].rearrange("p (k n) -> p k n", n=384)
                    slot = c % GRP
                    for (c0, n) in tiles:
                        pss = [next_ps() for _ in range(3)]
                        for s in range(3):
                            mm_group(pss[s][0][:, :n], [(wv[:, k, s * 128:(s + 1) * 128], hT[:, k, c0:c0 + n]) for k in range(KC)],
                                     reads=[Bw] + hT_bufs(c0, n), writes=[pss[s][1]])
                        (psb, Bpb), (psc, Bpc), (psh, Bph) = pss
                        gcs, Bg = next_tmp()
                        P.op(act, lambda gcs=gcs, psc=psc: nc.scalar.copy(gcs[:, :n], psc[:, :n]), reads=[Bpc], writes=[Bg])
                        u, Bu = next_tmp()
                        P.op(dve, lambda u=u, psh=psh, gcs=gcs: nc.vector.tensor_tensor(u[:, :n], psh[:, :n], gcs[:, :n], ALU.mult),
                             reads=[Bph, Bg], writes=[Bu])
                        cv, Bc = next_tmp()
                        P.op(act, lambda cv=cv, u=u: nc.scalar.activation(out=cv[:, :n], in_=u[:, :n], func=AF.Copy,
                                                                             scale=convw[:, (j * 3 + 1) * KC + c:(j * 3 + 1) * KC + c + 1]),
                             reads=[Bu, B_par], writes=[Bc])
                        wdt = 256 if c0 < TCX else 64
                        u3 = u[:, :n].rearrange("p (r w) -> p r w", w=wdt)
                        c3v = cv[:, :n].rearrange("p (r w) -> p r w", w=wdt)
                        P.op(dve, lambda u3=u3, c3v=c3v: nc.vector.scalar_tensor_tensor(
                            out=c3v[:, :, 1:wdt], in0=u3[:, :, 0:wdt - 1], scalar=convw[:, (j * 3 + 0) * KC + c:(j * 3 + 0) * KC + c + 1], in1=c3v[:, :, 1:wdt],
                            op0=ALU.mult, op1=ALU.add), reads=[Bu, Bc, B_par], writes=[Bc])
                        P.op(dve, lambda u3=u3, c3v=c3v: nc.vector.scalar_tensor_tensor(
                            out=c3v[:, :, 0:wdt - 1], in0=u3[:, :, 1:wdt], scalar=convw[:, (j * 3 + 2) * KC + c:(j * 3 + 2) * KC + c + 1], in1=c3v[:, :, 0:wdt - 1],
                            op0=ALU.mult, op1=ALU.add), reads=[Bu, Bc, B_par], writes=[Bc])
                        P.op(dve, lambda psb=psb, cv=cv: nc.vector.tensor_tensor(zb[:, slot, c0:c0 + n], psb[:, :n], cv[:, :n], ALU.mult),
                             reads=[Bpb, Bc], writes=[B_z[slot]])
                return (load, comp)

            def mk_out(g0, G):
                def load(w, Bw):
                    wv = w[:, 0:G * 1024].rearrange("p (k n) -> p k n", n=1024)
                    P.dma(pool, [(wv, conv_w_out_d[j, g0 * 128:(g0 + G) * 128, :].rearrange("(k p) n -> p k n", p=128))], writes=[Bw])

                def comp(w, Bw):
                    wv = w[:, 0:G * 1024].rearrange("p (k n) -> p k n", n=1024)
                    wout_compute(wv, Bw, G, tiles, l, b, 2)
                return (load, comp)
            for c in range(0, 6):
                steps.append(mk_in(c))
            steps.append(mk_out(0, 6))
            for c in range(6, 8):
                steps.append(mk_in(c))
            steps.append(mk_out(6, 2))
            return steps

        def ffn_steps(l, b, tiles):
            steps = []

            def mk_in(f):
                def load(w, Bw):
                    wv = w[:, 0:2048].rearrange("p (k n) -> p k n", n=256)
                    P.dma(pool, [(wv[:, :, s * 128:(s + 1) * 128],
                                  ffn_w_in_d[l, :, s * DFF + f * 128:s * DFF + (f + 1) * 128].rearrange("(k p) n -> p k n", p=128))
                                 for s in range(2)], writes=[Bw])

                def comp(w, Bw):
                    wv = w[:, 0:2048].rearrange("p (k n) -> p k n", n=256)
                    slot = f % GRP
                    for (c0, n) in tiles:
                        (psg, Bpg), (psu, Bpu) = next_ps(), next_ps()
                        mm_group(psg[:, :n], [(wv[:, k, 0:128], hT[:, k, c0:c0 + n]) for k in range(KC)],
                                 reads=[Bw] + hT_bufs(c0, n), writes=[Bpg])
                        mm_group(psu[:, :n], [(wv[:, k, 128:256], hT[:, k, c0:c0 + n]) for k in range(KC)],
                                 reads=[Bw] + hT_bufs(c0, n), writes=[Bpu])
                        sg, Bs = next_tmp()
                        P.op(act, lambda sg=sg, psg=psg: nc.scalar.activation(out=sg[:, :n], in_=psg[:, :n], func=AF.Silu),
                             reads=[Bpg], writes=[Bs])
                        P.op(dve, lambda sg=sg, psu=psu: nc.vector.tensor_tensor(zb[:, slot, c0:c0 + n], psu[:, :n], sg[:, :n], ALU.mult),
                             reads=[Bpu, Bs], writes=[B_z[slot]])
                return (load, comp)

            def mk_out(g0, G):
                def load(w, Bw):
                    wv = w[:, 0:G * 1024].rearrange("p (k n) -> p k n", n=1024)
                    P.dma(pool, [(wv, ffn_w_out_d[l, g0 * 128:(g0 + G) * 128, :].rearrange("(k p) n -> p k n", p=128))], writes=[Bw])

                def comp(w, Bw):
                    wv = w[:, 0:G * 1024].rearrange("p (k n) -> p k n", n=1024)
                    wout_compute(wv, Bw, G, tiles, l, b, 5)
                return (load, comp)
            for g0 in range(0, NF, GRP):
                G = min(GRP, NF - g0)
                for f in range(g0, g0 + G):
                    steps.append(mk_in(f))
                steps.append(mk_out(g0, G))
            return steps

        LNSC = float(np.log(128.0 ** -0.5))

        def gla_phase(l, j, b, last):
            st0, Bst0 = next_stage()
            st1, Bst1 = next_stage()
            for z, (st, Bst) in enumerate(((st0, Bst0), (st1, Bst1))):
                P.dma(sp, [(st[0:16, :], gla_wa2_d[j, z, :, :]), (st[16:17, :], gla_ba_d[j, z:z + 1, :])], writes=[Bst])
                P.op(dve, lambda st=st, z=z: nc.vector.tensor_copy(wa2[0:17, z, :], st[0:17, :]), reads=[Bst], writes=[B_wa])
            P.dma(pool, [(wa1[:, :, z * 16:(z + 1) * 16], gla_wa1_d[j, z, :, :].rearrange("(k p) r -> p k r", p=128)) for z in range(2)],
                  writes=[B_wa], sembuf=B_wa)
            P.dma(sp, [(gng[:], gla_ng_d[j, :, :])], writes=[B_gl], sembuf=B_gl)
            for z in range(2):
                P.op(dve, lambda z=z: nc.vector.memset(r1aug[z], 1.0), writes=[B_r1[z]])
            for (c0, n) in ALL_TILES:
                for z in range(2):
                    ps, Bps = next_ps()
                    mm_group(ps[0:16, :n], [(wa1[:, k, z * 16:(z + 1) * 16], hT[:, k, c0:c0 + n]) for k in range(KC)],
                             reads=[B_wa] + hT_bufs(c0, n), writes=[Bps])
                    P.op(act, lambda ps=ps, z=z, c0=c0, n=n: nc.scalar.copy(r1aug[z][0:16, c0:c0 + n], ps[0:16, :n]),
                         reads=[Bps], writes=[B_r1[z]])

            def head_steps(h):
                def load_in(w, Bw):
                    wv = w[:, 0:6144].rearrange("p (k n) -> p k n", n=768)
                    srcs = [(0, 128, h * 128), (128, 128, 512 + h * 128), (256, 256, 2048 + h * 256), (512, 256, 1024 + h * 256)]
                    P.dma(pool, [(wv[:, :, o:o + wd], gla_w_in_d[j, :, s:s + wd].rearrange("(k p) n -> p k n", p=128))
                                 for (o, wd, s) in srcs], writes=[Bw])

                def load_out(w, Bw):
                    wv = w[:, 0:2048].rearrange("p (k n) -> p k n", n=1024)
                    P.dma(pool, [(wv, gla_w_out_d[j, h * 256:(h + 1) * 256, :].rearrange("(k p) n -> p k n", p=128))], writes=[Bw])

                state = {}

                def comp_in(w, Bw):
                    state["w"] = (w, Bw)

                def comp_out(wo, Bwo):
                    w, Bw = state["w"]
                    wv = w[:, 0:6144].rearrange("p (k n) -> p k n", n=768)
                    wov = wo[:, 0:2048].rearrange("p (k n) -> p k n", n=1024)
                    P.dma(sp, [(brh[:], gla_br_d[j, :, h * 256:(h + 1) * 256])], writes=[B_gl], sembuf=B_gl)
                    P.op(dve, lambda: nc.vector.memset(Sst[:], 0.0), writes=[B_S])
                    order = [1, 0] + list(range(NCH - 1, 1, -1))
                    for idx, n in enumerate(order):
                        cs = slice(n * 128, (n + 1) * 128)
                        P.op(act, lambda n=n: nc.scalar.copy(sbprev[:, n, :], Sst[:]), reads=[B_S], writes=[B_sbp])
                        psk, Bpk = next_ps()
                        mm_group(psk[:, 0:128], [(hT[:, k, cs], wv[:, k, 128:256]) for k in range(KC)], reads=[Bw, B_hT[n]], writes=[Bpk])
                        psv_, Bpv = next_ps()
                        mm_group(psv_[:, 0:256], [(hT[:, k, cs], wv[:, k, 512:768]) for k in range(KC)], reads=[Bw, B_hT[n]], writes=[Bpv])
                        psl, Bpl = next_ps()
                        mm_group(psl[:, 0:128], [(r1aug[1][0:17, cs], wa2[0:17, 1, h * 128:(h + 1) * 128])], reads=[B_r1[1], B_wa], writes=[Bpl])
                        e1, Be1 = next_tmp()
                        P.op(act, lambda e1=e1, psl=psl: nc.scalar.activation(out=e1[:, 0:128], in_=psl[:, 0:128], func=AF.Exp, scale=-1.0),
                             reads=[Bpl], writes=[Be1])
                        P.op(act, lambda e1=e1: nc.scalar.activation(out=e1[:, 0:128], in_=e1[:, 0:128], func=AF.Ln, bias=1.0),
                             reads=[Be1], writes=[Be1])
                        P.op(act, lambda n=n, psv_=psv_: nc.scalar.copy(vtm[:, n, :], psv_[:, 0:256]), reads=[Bpv], writes=[B_vtm])
                        if idx == len(order) - 1:
                            break
                        psc, Bpc = next_ps()
                        mm_group(psc[:, 0:128], [(M4, e1[:, 0:128])], reads=[B_cst, Be1], writes=[Bpc])
                        mm_group(psc[:, 128:129], [(e1[:, 0:128], negcol)], reads=[B_cst, Be1], writes=[Bpc])
                        P.op(act, lambda psc=psc: nc.scalar.activation(out=ec_t[:], in_=psc[:, 0:128], func=AF.Exp), reads=[Bpc], writes=[B_ec])
                        P.op(act, lambda psc=psc: nc.scalar.activation(out=dec_t[:, 0:1], in_=psc[:, 128:129], func=AF.Exp), reads=[Bpc], writes=[B_dec])
                        kd, Bkd = kdec_t[idx % 2], B_kdec[idx % 2]
                        P.op(dve, lambda kd=kd, psk=psk: nc.vector.tensor_tensor(kd[:], psk[:, 0:128], ec_t[:], ALU.mult),
                             reads=[Bpk, B_ec], writes=[Bkd])
                        psu, Bpu = next_ps()
                        mm_group(psu[:, 0:256], [(kd[:], vtm[:, n, :])], reads=[Bkd, B_vtm], writes=[Bpu])
                        P.op(dve, lambda psu=psu: nc.vector.scalar_tensor_tensor(out=Sst[:], in0=Sst[:], scalar=dec_t[:, 0:1], in1=psu[:, 0:256],
                                                                                op0=ALU.mult, op1=ALU.add),
                             reads=[B_S, B_dec, Bpu], writes=[B_S])
                    P.op(dve, lambda: nc.vector.memset(Sst[:], 0.0), writes=[B_S])
                    for n in range(NCH):
                        cs = slice(n * 128, (n + 1) * 128)
                        outn = not (last and n < 2)
                        r = 2 if n < 2 else b
                        sfb, Bsfb = Sbf[n % 2], B_Sbf[n % 2]
                        P.op(act, lambda sfb=sfb: nc.scalar.copy(sfb[:], Sst[:]), reads=[B_S], writes=[Bsfb])
                        pskr, Bpkr = next_ps()
                        ncol = 384 if outn else 128
                        mm_group(pskr[:, 0:ncol], [(hT[:, k, cs], wv[:, k, 128:128 + ncol]) for k in range(KC)], reads=[Bw, B_hT[n]], writes=[Bpkr])
                        psl, Bpl = next_ps()
                        nz = 2 if outn else 1
                        for z in range(nz):
                            mm_group(psl[:, z * 128:(z + 1) * 128], [(r1aug[z][0:17, cs], wa2[0:17, z, h * 128:(h + 1) * 128])],
                                     reads=[B_r1[z], B_wa], writes=[Bpl])
                        e1, Be1 = next_tmp()
                        P.op(act, lambda e1=e1, psl=psl, nz=nz: nc.scalar.activation(out=e1[:, 0:128 * nz], in_=psl[:, 0:128 * nz], func=AF.Exp, scale=-1.0),
                             reads=[Bpl], writes=[Be1])
                        P.op(act, lambda e1=e1, nz=nz: nc.scalar.activation(out=e1[:, 0:128 * nz], in_=e1[:, 0:128 * nz], func=AF.Ln, bias=1.0),
                             reads=[Be1], writes=[Be1])
                        psc, Bpc = next_ps()
                        mm_group(psc[:, 0:128], [(M2, e1[:, 0:128])], reads=[B_cst, Be1], writes=[Bpc])
                        mm_group(psc[:, 128:256], [(e1[:, 0:128], M1)], reads=[B_cst, Be1], writes=[Bpc])
                        if outn:
                            mm_group(psc[:, 256:384], [(e1[:, 128:256], M3)], reads=[B_cst, Be1], writes=[Bpc])
                        P.op(act, lambda psc=psc: nc.scalar.activation(out=ec_t[:], in_=psc[:, 0:128], func=AF.Exp), reads=[Bpc], writes=[B_ec])
                        P.op(act, lambda psc=psc: nc.scalar.activation(out=dec_t[:, 1:2], in_=psc[:, 255:256], func=AF.Exp), reads=[Bpc], writes=[B_dec])
                        kd, Bkd = kdec_t[n % 2], B_kdec[n % 2]
                        P.op(dve, lambda kd=kd, pskr=pskr: nc.vector.tensor_tensor(kd[:], pskr[:, 0:128], ec_t[:], ALU.mult),
                             reads=[Bpkr, B_ec], writes=[Bkd])
                        if outn:
                            P.op(act, lambda psc=psc: nc.scalar.activation(out=Eq_t[:], in_=psc[:, 128:384], func=AF.Exp, bias=LNSC),
                                 reads=[Bpc], writes=[B_Eq])
                            P.op(act, lambda psc=psc: nc.scalar.activation(out=Ek_t[:], in_=psc[:, 128:384], func=AF.Exp, scale=-1.0),
                                 reads=[Bpc], writes=[B_Ek])
                            psqk, Bpqk = next_ps()
                            mm_group(psqk[:, 0:128], [(wv[:, k, 0:128], hT[:, k, cs]) for k in range(KC)], reads=[Bw, B_hT[n]], writes=[Bpqk])
                            mm_group(psqk[:, 128:256], [(wv[:, k, 128:256], hT[:, k, cs]) for k in range(KC)], reads=[Bw, B_hT[n]], writes=[Bpqk])
                            for z in range(2):
                                P.op(dve, lambda z=z, psqk=psqk: nc.vector.tensor_tensor(qdec_t[:, z, :], psqk[:, 0:128], Eq_t[:, z * 128:(z + 1) * 128], ALU.mult),
                                     reads=[Bpqk, B_Eq], writes=[B_qdec])
                                P.op(dve, lambda z=z, psqk=psqk: nc.vector.tensor_tensor(kinv_t[:, z, :], psqk[:, 128:256], Ek_t[:, z * 128:(z + 1) * 128], ALU.mult),
                                     reads=[Bpqk, B_Ek], writes=[B_kinv])
                            pss, Bpss = next_ps()
                            for z in range(2):
                                mm_group(pss[:, z * 128:(z + 1) * 128], [(kinv_t[:, z, :], qdec_t[:, z, :])], reads=[B_kinv, B_qdec], writes=[Bpss])
                            P.op(dve, lambda pss=pss: nc.vector.tensor_tensor(sm_t[:].rearrange("p (z c) -> p z c", z=2),
                                                                              pss[:, 0:256].rearrange("p (z c) -> p z c", z=2), mask2, ALU.mult),
                                 reads=[Bpss, B_cst], writes=[B_sm])
                            pso, Bpo = next_ps()
                            mm_group(pso[:, 0:256], [(sm_t[:, 0:128], vtm[:, n, :]), (qdec_t[:, 0, :], sfb[:]),
                                                     (sm_t[:, 128:256], vtm[:, n, :]), (qdec_t[:, 1, :], sbprev[:, n, :])],
                                     reads=[B_sm, B_vtm, B_qdec, Bsfb, B_sbp], writes=[Bpo])
                        if n < NCH - 1:
                            psu, Bpu = next_ps()
                            mm_group(psu[:, 0:256], [(kd[:], vtm[:, n, :])], reads=[Bkd, B_vtm], writes=[Bpu])
                            P.op(dve, lambda psu=psu: nc.vector.scalar_tensor_tensor(out=Sst[:], in0=Sst[:], scalar=dec_t[:, 1:2], in1=psu[:, 0:256],
                                                                                    op0=ALU.mult, op1=ALU.add),
                                 reads=[B_S, B_dec, Bpu], writes=[B_S])
                        if not outn:
                            continue
                        junk, Bj = next_tmp()
                        P.op(act, lambda junk=junk, pso=pso: nc.scalar.activation(out=junk[:, 0:256], in_=pso[:, 0:256], func=AF.Square, accum_out=dec_t[:, 2:3]),
                             reads=[Bpo], writes=[Bj, B_dec])
                        P.op(act, lambda: nc.scalar.activation(out=dec_t[:, 3:4], in_=dec_t[:, 2:3], func=AF.Ln, bias=EPS, scale=1.0 / 256),
                             reads=[B_dec], writes=[B_dec])
                        P.op(act, lambda: nc.scalar.activation(out=dec_t[:, 3:4], in_=dec_t[:, 3:4], func=AF.Exp, scale=-0.5),
                             reads=[B_dec], writes=[B_dec])
                        rr, Brr = next_tmp()
                        P.op(dve, lambda rr=rr, pskr=pskr: nc.vector.tensor_tensor(rr[:, 0:256], pskr[:, 128:384], brh[:], ALU.add),
                             reads=[Bpkr, B_gl], writes=[Brr])
                        P.op(act, lambda rr=rr: nc.scalar.activation(out=rr[:, 256:512], in_=rr[:, 0:256], func=AF.Silu), reads=[Brr], writes=[Brr])
                        t1, Bt1 = next_tmp()
                        P.op(dve, lambda t1=t1, pso=pso: nc.vector.scalar_tensor_tensor(out=t1[:, 0:256], in0=pso[:, 0:256], scalar=dec_t[:, 3:4], in1=gng[:],
                                                                                     op0=ALU.mult, op1=ALU.mult),
                             reads=[Bpo, B_dec, B_gl], writes=[Bt1])
                        P.op(dve, lambda t1=t1, rr=rr: nc.vector.tensor_tensor(og_t[:], t1[:, 0:256], rr[:, 256:512], ALU.mult),
                             reads=[Bt1, Brr], writes=[B_og])

                        def tr():
                            nc.tensor.transpose(ptb[:, 0:128], og_t[:, 0:128], ident_b[:])
                            return nc.tensor.transpose(ptb[:, 128:256], og_t[:, 128:256], ident_b[:])
                        P.op(pe, tr, reads=[B_og, B_idb], writes=[B_ptb])
                        P.op(act, lambda: nc.scalar.copy(ogT_t[:].rearrange("p j t -> p (j t)"), ptb[:, 0:256]), reads=[B_ptb], writes=[B_ogT])
                        for half in range(2):
                            psy, Bpy = next_ps()
                            for q in range(4):
                                cp = half * 4 + q
                                mm_group(psy[:, q * 128:(q + 1) * 128], [(wov[:, jj, cp * 128:(cp + 1) * 128], ogT_t[:, jj, :]) for jj in range(2)],
                                         reads=[Bwo, B_ogT], writes=[Bpy])
                            for q in range(4):
                                cp = half * 4 + q
                                P.op(dve, lambda psy=psy, q=q, cp=cp, r=r, cs=cs: nc.vector.scalar_tensor_tensor(
                                    out=xs[:, cp, cs], in0=psy[:, q * 128:(q + 1) * 128], scalar=ada_ap(l, r, 2, cp), in1=xs[:, cp, cs],
                                    op0=ALU.mult, op1=ALU.add), reads=[Bpy, B_ada, B_xs[cp][n]], writes=[B_xs[cp][n]])
                return [(load_in, comp_in), (load_out, comp_out)]
            steps = []
            for h in range(4):
                steps += head_steps(h)
            return steps

        def final_phase(b):
            for (c0, n) in LAT_TILES:
                rstd_tile(c0, n)
                for k in range(KC):
                    t, Bt = next_tmp()
                    P.op(dve, lambda t=t, k=k: nc.vector.tensor_tensor(t[:, :n], xs[:, k, c0:c0 + n], rs[:, :n], ALU.mult),
                         reads=xs_bufs(k, c0, n) + [B_rs], writes=[Bt])
                    P.op(act, lambda t=t, k=k: nc.scalar.activation(out=xs[:, k, c0:c0 + n], in_=t[:, :n], func=AF.Copy, scale=fing[:, k:k + 1]),
                         reads=[Bt, B_par], writes=xs_bufs(k, c0, n))

        if "ada" in phases:
            run_steps(ada_steps())
        for b in range(2):
            load_x(b)
            for l in layers:
                last = (l == DEPTH - 1)
                kind = l % 2
                j = l // 2
                if "norm1" in phases:
                    norm_phase(l, 0, b, ALL_TILES)
                if "mixer" in phases:
                    if kind == 0:
                        run_steps(conv_steps(l, j, b))
                    else:
                        run_steps(gla_phase(l, j, b, last), dist=1)
                tiles = LAT_TILES if last else ALL_TILES
                if "norm2" in phases:
                    norm_phase(l, 1, b, tiles)
                if "ffn" in phases:
                    run_steps(ffn_steps(l, b, tiles))
            if final:
                final_phase(b)
            store_x(b, range(2, NCH), xo_d, 2)
            if want_ctx_out:
                store_x(b, range(0, 2), ctxo_d, 0)
        for Bst in B_stage:
            if Bst.dsem is not None:
                sp.e.wait_ge(Bst.dsem[0], Bst.dsem[1])
    return nc


def _consts():
    s = np.arange(128)[:, None]
    t = np.arange(128)[None, :]
    c = np.zeros((128, 8, 128), np.float32)
    c[:, 0, :] = np.eye(128, dtype=np.float32)
    c[:, 1, :] = np.where(s <= t, -1.0 / 16, 0.0)
    c[:, 2, :] = np.where(s > t, -1.0 / 16, 0.0)
    c[:, 3, :] = np.where(s >= t, -1.0 / 16, 0.0)
    c[:, 4, :] = np.where(s < t, -1.0 / 16, 0.0)
    c[:, 5, :] = np.where(s <= t, 1.0, 0.0)
    c[:, 6, :] = np.where(s >= t, 1.0, 0.0)
    c[:, 7, :] = -1.0 / 16
    return c


def _fm(v):
    v = np.asarray(v, np.float32)
    lead = v.shape[:-1]
    a = v.reshape(lead + (KC, 128))
    a = np.moveaxis(a, -1, 0)
    return np.ascontiguousarray(a)


def make_in_maps(x, c, ctx, c_ctx, ada_w, ada_b, norm1_g, norm2_g, conv_w_in, conv_w, conv_w_out,
                 gla_w_in, gla_b_r, gla_w_a1, gla_w_a2, gla_b_a, gla_norm_g, gla_w_out,
                 ffn_w_in, ffn_w_out, final_g):
    f = lambda a: np.ascontiguousarray(np.asarray(a, np.float32))
    shared = {
        "ada_w": f(ada_w),
        "ada_b": np.ascontiguousarray(np.moveaxis(f(ada_b).reshape(DEPTH, 48, 128), -1, 0)),
        "n1g": _fm(norm1_g), "n2g": _fm(norm2_g),
        "conv_w_in": f(conv_w_in), "convw": _fm(conv_w), "conv_w_out": f(conv_w_out),
        "gla_w_in": f(gla_w_in),
        "gla_br": np.ascontiguousarray(np.broadcast_to(f(gla_b_r)[:, None, :], (2, 128, D))),
        "gla_wa1": f(gla_w_a1), "gla_wa2": f(gla_w_a2), "gla_ba": f(gla_b_a),
        "gla_ng": np.ascontiguousarray(np.broadcast_to(f(gla_norm_g)[:, None, :], (2, 128, 256))),
        "gla_w_out": f(gla_w_out), "ffn_w_in": f(ffn_w_in), "ffn_w_out": f(ffn_w_out),
        "fing": _fm(final_g), "cst": _consts(),
    }
    x = f(x)
    c = f(c)
    ctx = f(ctx)
    c_ctx = f(c_ctx)
    maps = []
    for i in range(NCORES):
        c3 = np.stack([c[2 * i], c[2 * i + 1], c_ctx], 0)
        m = dict(shared)
        m["x"] = np.ascontiguousarray(x[2 * i:2 * i + 2])
        m["ctx"] = np.ascontiguousarray(ctx[2 * i:2 * i + 2])
        m["c3"] = np.ascontiguousarray(np.moveaxis(c3.reshape(3, KC, 128), (0, 1, 2), (2, 1, 0)))
        maps.append(m)
    return maps


_NC_CACHE = {}


def kernel(**inputs):
    maps = make_in_maps(**inputs)
    if "full" not in _NC_CACHE:
        _NC_CACHE["full"] = build_program()
    nc = _NC_CACHE["full"]
    outs = []
    for g in range(NCORES // CORES_PER_LAUNCH):
        sub = maps[g * CORES_PER_LAUNCH:(g + 1) * CORES_PER_LAUNCH]
        res = run_bass_kernel_spmd(nc, sub, core_ids=list(range(CORES_PER_LAUNCH)))
        outs += [np.asarray(r["xo"], np.float32) for r in res.results]
    return np.concatenate(outs, axis=0)
```

```python
import numpy as np
from contextlib import ExitStack
import concourse.bass as bass
import concourse.mybir as mybir
from concourse.bass_utils import run_bass_kernel_spmd

F32 = mybir.dt.float32
BF16 = mybir.dt.bfloat16
AF = mybir.ActivationFunctionType
ALU = mybir.AluOpType

D = 1024
KC = 8
TCX = 256
TL = 2048
T = TCX + TL
NCH = T // 128
DFF = 2816
NF = DFF // 128
DEPTH = 4
EPS = 1e-6
NCORES = 8
CORES_PER_LAUNCH = 4
GRP = 6

ALL_TILES = [(0, 256)] + [(256 + 512 * i, 512) for i in range(4)]
LAT_TILES = ALL_TILES[1:]


class Buf:
    _reg = []

    def __init__(self, name, region=None):
        self.name = name
        self.w = None
        self.r = {}
        self.ov = []
        self.region = region
        self.dsem = None
        if region is not None:
            for o in Buf._reg:
                if o.region[0] == region[0] and o.region[1] < region[2] and region[1] < o.region[2]:
                    o.ov.append(self)
                    self.ov.append(o)
            Buf._reg.append(self)


class Q:
    def __init__(self, name, eng, sem):
        self.name, self.e, self.sem, self.cnt, self.waited = name, eng, sem, 0, {}


class Prog:
    def __init__(self, nc, es):
        self.nc = nc
        self.es = es
        self.nsem = 0
        mk = lambda n, e: Q(n, e, self.sem(n))
        self.pe = mk("pe", nc.tensor)
        self.act = mk("act", nc.scalar)
        self.dve = mk("dve", nc.vector)
        self.pool = mk("pool", nc.gpsimd)
        self.sp = mk("sp", nc.sync)

    def sem(self, name):
        self.nsem += 1
        return self.es.enter_context(self.nc.semaphore(f"s_{name}_{self.nsem}"))

    def sb(self, name, shape, dt):
        return self.es.enter_context(self.nc.sbuf_tensor("t_" + name, shape, dt))

    def _deps(self, reads, writes):
        d = {}

        def add(t):
            if t is not None:
                if d.get(t[0], 0) < t[1]:
                    d[t[0]] = t[1]

        for b in reads:
            add(b.w)
            for o in b.ov:
                add(o.w)
        for b in writes:
            for bb in [b] + b.ov:
                add(bb.w)
                for s, v in bb.r.items():
                    add((s, v))
        return d

    def _wait(self, q, d):
        for s, v in d.items():
            if q.name == "pe" and s is q.sem:
                continue
            if q.waited.get(s, 0) < v:
                q.e.wait_ge(s, v)
                q.waited[s] = v

    def op(self, q, fn, reads=(), writes=()):
        self._wait(q, self._deps(reads, writes))
        ins = fn()
        q.cnt += 1
        ins.then_inc(q.sem, 1)
        tag = (q.sem, q.cnt)
        for b in writes:
            b.w = tag
            b.r = {}
        for b in reads:
            b.r[q.sem] = q.cnt

    def dma(self, q, pairs, reads=(), writes=(), sembuf=None):
        sb = sembuf if sembuf is not None else (writes[0] if writes else reads[0])
        if sb.dsem is None:
            sb.dsem = [self.sem("d" + sb.name), 0]
        self._wait(q, self._deps(reads, writes))
        for o, i in pairs:
            q.e.dma_start(out=o, in_=i).then_inc(sb.dsem[0], 16)
            sb.dsem[1] += 16
        tag = (sb.dsem[0], sb.dsem[1])
        for b in writes:
            b.w = tag
            b.r = {}
        for b in reads:
            b.r[tag[0]] = tag[1]
        return tag


def build_program(layers=(0, 1, 2, 3), first=True, final=True, phases=("ada", "norm1", "mixer", "norm2", "ffn"), nb=2):
    Buf._reg = []
    NR = nb + 1
    nc = bass.Bass("TRN2", target_bir_lowering=False)

    def din(name, shape):
        return nc.dram_tensor(name, list(shape), F32, kind="ExternalInput").ap()

    x_d = din("x", (nb, TL, D))
    ctx_d = din("ctx", (nb, TCX, D))
    c3_d = din("c3", (128, KC, NR))
    ada_w_d = din("ada_w", (DEPTH, D, 6 * D))
    ada_b_d = din("ada_b", (128, DEPTH, 48))
    n1g_d = din("n1g", (128, DEPTH, KC))
    n2g_d = din("n2g", (128, DEPTH, KC))
    conv_w_in_d = din("conv_w_in", (2, D, 3 * D))
    convw_d = din("convw", (128, 2, 3, KC))
    conv_w_out_d = din("conv_w_out", (2, D, D))
    gla_w_in_d = din("gla_w_in", (2, D, 3 * D))
    gla_br_d = din("gla_br", (2, 128, D))
    gla_wa1_d = din("gla_wa1", (2, 2, D, 16))
    gla_wa2_d = din("gla_wa2", (2, 2, 16, 512))
    gla_ba_d = din("gla_ba", (2, 2, 512))
    gla_ng_d = din("gla_ng", (2, 128, 256))
    gla_w_out_d = din("gla_w_out", (2, D, D))
    ffn_w_in_d = din("ffn_w_in", (DEPTH, D, 2 * DFF))
    ffn_w_out_d = din("ffn_w_out", (DEPTH, DFF, D))
    fing_d = din("fing", (128, KC))
    cst_d = din("cst", (128, 8, 128))
    xo_d = nc.dram_tensor("xo", [nb, TL, D], F32, kind="ExternalOutput").ap()
    want_ctx_out = not final
    if want_ctx_out:
        ctxo_d = nc.dram_tensor("ctxo", [nb, TCX, D], F32, kind="ExternalOutput").ap()

    with ExitStack() as es:
        P = Prog(nc, es)
        pe, act, dve, pool, sp = P.pe, P.act, P.dve, P.pool, P.sp

        xs = P.sb("xs", [128, KC, T], F32)
        hT = P.sb("hT", [128, KC, T], BF16)
        zflat = P.sb("zb", [128, GRP * T], BF16)
        zb = zflat[:, :].rearrange("p (k t) -> p k t", t=T)
        Wt = [P.sb(f"W{i}", [128, 6144], BF16) for i in range(3)]
        stage = [P.sb(f"stage{i}", [128, 512], F32) for i in range(2)]
        tmpF = [P.sb(f"tmpF{i}", [128, 512], F32) for i in range(3)]
        sqb = [P.sb(f"sq{i}", [128, 512], BF16) for i in range(2)]
        rs = P.sb("rs", [128, 512], F32)
        cst = P.sb("cst", [128, 8, 128], F32)
        ident_b = P.sb("ident_b", [128, 128], BF16)
        ones_b = P.sb("ones_b", [128, 128], BF16)
        ada_b_t = P.sb("ada_b_t", [128, DEPTH, 48], F32)
        ada_sb = P.sb("ada_sb", [128, DEPTH * NR * 48], F32)
        n1g = P.sb("n1g", [128, DEPTH, KC], F32)
        n2g = P.sb("n2g", [128, DEPTH, KC], F32)
        convw = P.sb("convw", [128, 2 * 3 * KC], F32)
        fing = P.sb("fing", [128, KC], F32)
        c3f = P.sb("c3f", [128, KC, NR], F32)
        c3b = P.sb("c3b", [128, KC, NR], BF16)
        brh = P.sb("brh", [128, 256], F32)
        gng = P.sb("gng", [128, 256], F32)
        wa1 = P.sb("wa1", [128, KC, 32], BF16)
        wa2 = P.sb("wa2", [17, 2, 512], BF16)
        ec_t = P.sb("ec_t", [128, 128], F32)
        Eq_t = P.sb("Eq_t", [128, 256], F32)
        Ek_t = P.sb("Ek_t", [128, 256], F32)
        kdec_t = [P.sb(f"kdec{i}", [128, 128], BF16) for i in range(2)]
        qdec_t = P.sb("qdec", [128, 2, 128], BF16)
        kinv_t = P.sb("kinv", [128, 2, 128], BF16)
        sm_t = P.sb("sm", [128, 256], BF16)
        Sst = P.sb("Sst", [128, 256], F32)
        Sbf = [P.sb(f"Sbf{i}", [128, 256], BF16) for i in range(2)]
        dec_t = P.sb("dec_t", [128, 4], F32)
        og_t = P.sb("og_t", [128, 256], BF16)
        ogT_t = P.sb("ogT_t", [128, 2, 128], BF16)

        NPS = 7
        psf = [es.enter_context(nc.psum_tensor(f"ps{i}", [128, 512], F32)) for i in range(NPS)]
        ptb = es.enter_context(nc.psum_tensor("ptb", [128, 1024], BF16))

        B_xs = [[Buf(f"xs{k}_{n}") for n in range(NCH)] for k in range(KC)]
        B_hT = [Buf(f"hT{n}") for n in range(NCH)]
        B_z = [Buf(f"z{i}", ("z", i * T, (i + 1) * T)) for i in range(GRP)]
        B_W = [Buf(f"W{i}") for i in range(3)]
        B_stage = [Buf(f"stage{i}") for i in range(2)]
        B_tmp = [Buf(f"tmpF{i}") for i in range(3)]
        B_sq = [Buf(f"sq{i}") for i in range(2)]
        B_rs = Buf("rs")
        B_ps = [Buf(f"ps{i}") for i in range(NPS)]
        B_ptb = Buf("ptb")
        B_cst = Buf("cst")
        B_par = Buf("params")
        B_ada = Buf("ada")
        B_c3b = Buf("c3b")
        B_gl = Buf("glasmall")
        B_wa = Buf("wa")
        B_ec, B_Eq, B_Ek = Buf("ec"), Buf("Eq"), Buf("Ek")
        B_kdec = [Buf("kdec0"), Buf("kdec1")]
        B_qdec, B_kinv, B_sm = Buf("qdec"), Buf("kinv"), Buf("sm")
        B_S = Buf("Sst")
        B_Sbf = [Buf("Sbf0"), Buf("Sbf1")]
        B_dec = Buf("dec")
        B_og, B_ogT = Buf("og"), Buf("ogT")
        SB0, V0, R1F0, R1B0 = 0, 4608, 9216, 11520
        B_sbp = Buf("sbprev", ("z", SB0, V0))
        B_vtm = Buf("vtm", ("z", V0, R1F0))
        B_r1 = [Buf("r1f", ("z", R1F0, R1B0)), Buf("r1b", ("z", R1B0, R1B0 + T))]
        sbprev = zflat[:, SB0:V0].rearrange("p (n e) -> p n e", e=256)
        vtm = zflat[:, V0:R1F0].rearrange("p (n e) -> p n e", e=256)
        r1aug = [zflat[0:17, R1F0:R1B0], zflat[0:17, R1B0:R1B0 + T]]

        cnt = {"ps": 0, "tmp": 0, "sq": 0, "w": 0, "st": 0}

        def next_ps():
            i = cnt["ps"] % NPS
            cnt["ps"] += 1
            return psf[i], B_ps[i]

        def next_tmp():
            i = cnt["tmp"] % 3
            cnt["tmp"] += 1
            return tmpF[i], B_tmp[i]

        def next_sq():
            i = cnt["sq"] % 2
            cnt["sq"] += 1
            return sqb[i], B_sq[i]

        def next_w():
            i = cnt["w"] % 3
            cnt["w"] += 1
            return Wt[i], B_W[i]

        def next_stage():
            i = cnt["st"] % 2
            cnt["st"] += 1
            return stage[i], B_stage[i]

        def chunks_of(c0, n):
            return range(c0 // 128, (c0 + n) // 128)

        def xs_bufs(k, c0, n):
            return [B_xs[k][c] for c in chunks_of(c0, n)]

        def hT_bufs(c0, n):
            return [B_hT[c] for c in chunks_of(c0, n)]

        def mm_group(out_ap, pairs, reads, writes):
            def fn():
                ins = None
                for i, (l, r) in enumerate(pairs):
                    ins = nc.tensor.matmul(out_ap, l, r, start=(i == 0), stop=(i == len(pairs) - 1))
                return ins
            P.op(pe, fn, reads=reads, writes=writes)

        P.dma(sp, [(cst[:], cst_d[:, :, :])], writes=[B_cst])
        P.dma(sp, [(ada_b_t[:], ada_b_d[:, :, :]), (n1g[:], n1g_d[:, :, :]), (n2g[:], n2g_d[:, :, :]),
                   (convw[:], convw_d[:, :, :, :].rearrange("p j t k -> p (j t k)")), (fing[:], fing_d[:, :]), (c3f[:], c3_d[:, :, :])], writes=[B_par])
        ident_f = cst[:, 0, :]
        M1, M2, M3, M4 = cst[:, 1, :], cst[:, 2, :], cst[:, 3, :], cst[:, 4, :]
        mask2 = cst[:, 5:7, :]
        negcol = cst[:, 7, 0:1]
        B_idb = Buf("identb")
        P.op(dve, lambda: nc.vector.tensor_copy(ident_b[:], cst[:, 0, :]), reads=[B_cst], writes=[B_idb])
        P.op(dve, lambda: nc.vector.memset(ones_b[:], 1.0), writes=[B_idb])
        P.op(act, lambda: nc.scalar.activation(out=c3b[:], in_=c3f[:], func=AF.Silu), reads=[B_par], writes=[B_c3b])

        def ada_steps():
            steps = []
            holder = {}
            for l in layers:
                for blk in range(8):
                    def load(w, Bw, l=l, blk=blk):
                        wv = w[:, 0:6144].rearrange("p (k n) -> p k n", n=768)
                        P.dma(pool, [(wv, ada_w_d[l, :, blk * 768:(blk + 1) * 768].rearrange("(k p) n -> p k n", p=128))],
                              writes=[Bw])

                    def comp(w, Bw, l=l, blk=blk):
                        wv = w[:, 0:6144].rearrange("p (k n) -> p k n", n=768)
                        if blk == 0:
                            holder["ps"] = next_ps()
                        psA, BpsA = holder["ps"]
                        for jj in range(6):
                            j = blk * 6 + jj
                            mm_group(psA[:, j * NR:(j + 1) * NR],
                                     [(wv[:, k, jj * 128:(jj + 1) * 128], c3b[:, k, :]) for k in range(KC)],
                                     reads=[Bw, B_c3b], writes=[BpsA])
                        if blk < 7:
                            return
                        psv = psA[:, 0:48 * NR].rearrange("p (j r) -> p j r", r=NR)
                        for r in range(NR):
                            P.op(dve, lambda r=r: nc.vector.tensor_tensor(ada_sb[:, (l * NR + r) * 48:(l * NR + r + 1) * 48], psv[:, :, r], ada_b_t[:, l, :], ALU.add),
                                 reads=[BpsA, B_par], writes=[B_ada])
                        for r in range(NR):
                            for which, ng in ((0, n1g), (1, n2g)):
                                o0 = (l * NR + r) * 48 + 8 + 24 * which
                                sc = ada_sb[:, o0:o0 + 8]
                                P.op(dve, lambda sc=sc, ng=ng: nc.vector.scalar_tensor_tensor(
                                    out=sc, in0=sc, scalar=1.0, in1=ng[:, l, :], op0=ALU.add, op1=ALU.mult),
                                    reads=[B_ada, B_par], writes=[B_ada])
                    steps.append((load, comp))
            return steps

        def ada_ap(l, r, idx, k):
            o = (l * NR + r) * 48 + idx * 8 + k
            return ada_sb[:, o:o + 1]

        def load_x(b):
            for n in range(NCH):
                for half in range(2):
                    st, Bst = next_stage()
                    if n < 2:
                        src = ctx_d[b, n * 128:(n + 1) * 128, half * 512:(half + 1) * 512]
                    else:
                        src = x_d[b, (n - 2) * 128:(n - 1) * 128, half * 512:(half + 1) * 512]
                    P.dma(sp, [(st[:], src)], writes=[Bst])
                    ps, Bps = next_ps()

                    def fn(ps=ps, st=st):
                        ins = None
                        for q in range(4):
                            ins = nc.tensor.transpose(ps[:, q * 128:(q + 1) * 128], st[:, q * 128:(q + 1) * 128], ident_f)
                        return ins
                    P.op(pe, fn, reads=[Bst, B_cst], writes=[Bps])
                    dst = xs[:, half * 4:(half + 1) * 4, n * 128:(n + 1) * 128]
                    src_ps = ps[:, :].rearrange("p (q t) -> p q t", t=128)
                    wb = [B_xs[k][n] for k in range(half * 4, half * 4 + 4)]
                    if half == 0:
                        P.op(act, lambda dst=dst, s=src_ps: nc.scalar.copy(dst, s), reads=[Bps], writes=wb)
                    else:
                        P.op(dve, lambda dst=dst, s=src_ps: nc.vector.tensor_copy(dst, s), reads=[Bps], writes=wb)

        def store_x(b, chunks, dst_d, dst_off):
            for n in chunks:
                for half in range(2):
                    ps, Bps = next_ps()

                    def fn(ps=ps, n=n, half=half):
                        ins = None
                        for q in range(4):
                            ins = nc.tensor.transpose(ps[:, q * 128:(q + 1) * 128], xs[:, half * 4 + q, n * 128:(n + 1) * 128], ident_f)
                        return ins
                    P.op(pe, fn, reads=[B_xs[k][n] for k in range(half * 4, half * 4 + 4)] + [B_cst], writes=[Bps])
                    st, Bst = next_stage()
                    if half == 0:
                        P.op(act, lambda st=st, ps=ps: nc.scalar.copy(st[:], ps[:, :]), reads=[Bps], writes=[Bst])
                    else:
                        P.op(dve, lambda st=st, ps=ps: nc.vector.tensor_copy(st[:], ps[:, :]), reads=[Bps], writes=[Bst])
                    row = (n - dst_off) * 128
                    P.dma(sp, [(dst_d[b, row:row + 128, half * 512:(half + 1) * 512], st[:])], reads=[Bst], sembuf=Bst)

        def rstd_tile(c0, n):
            ps, Bps = next_ps()
            for k in range(KC):
                sq, Bsq = next_sq()
                xin = xs[:, k, c0:c0 + n]
                if k % 2 == 0:
                    P.op(act, lambda sq=sq, xin=xin: nc.scalar.activation(out=sq[:, :n], in_=xin, func=AF.Square),
                         reads=xs_bufs(k, c0, n), writes=[Bsq])
                else:
                    P.op(dve, lambda sq=sq, xin=xin: nc.vector.tensor_tensor(sq[:, :n], xin, xin, ALU.mult),
                         reads=xs_bufs(k, c0, n), writes=[Bsq])
                P.op(pe, lambda sq=sq, ps=ps, k=k: nc.tensor.matmul(ps[:, :n], ones_b[:], sq[:, :n], start=(k == 0), stop=(k == KC - 1)),
                     reads=[Bsq, B_idb], writes=[Bps])
            P.op(act, lambda ps=ps: nc.scalar.activation(out=rs[:, :n], in_=ps[:, :n], func=AF.Ln, bias=EPS, scale=1.0 / D),
                 reads=[Bps], writes=[B_rs])
            P.op(act, lambda: nc.scalar.activation(out=rs[:, :n], in_=rs[:, :n], func=AF.Exp, scale=-0.5),
                 reads=[B_rs], writes=[B_rs])

        def norm_phase(l, which, b, tiles):
            import os
            dbg = os.environ.get("NORMDBG", "")
            for (c0, n) in tiles:
                r = nb if c0 < TCX else b
                rstd_tile(c0, n)
                if dbg == "rstd":
                    continue
                for k in range(KC):
                    t, Bt = next_tmp()
                    P.op(dve, lambda t=t, k=k: nc.vector.tensor_tensor(t[:, :n], xs[:, k, c0:c0 + n], rs[:, :n], ALU.mult),
                         reads=xs_bufs(k, c0, n) + [B_rs], writes=[Bt])
                    P.op(act, lambda t=t, k=k, r=r: nc.scalar.activation(
                        out=hT[:, k, c0:c0 + n], in_=t[:, :n], func=AF.Identity,
                        bias=ada_ap(l, r, 3 * which, k), scale=ada_ap(l, r, 3 * which + 1, k)),
                        reads=[Bt, B_ada], writes=hT_bufs(c0, n))

        def wout_compute(wv, Bw, G, tiles, l, b, gidx):
            for cp in range(KC):
                for (c0, n) in tiles:
                    r = nb if c0 < TCX else b
                    ps, Bps = next_ps()
                    mm_group(ps[:, :n], [(wv[:, kk, cp * 128:(cp + 1) * 128], zb[:, kk, c0:c0 + n]) for kk in range(G)],
                             reads=[Bw] + B_z[:G], writes=[Bps])
                    xb = xs_bufs(cp, c0, n)
                    P.op(dve, lambda ps=ps, cp=cp, c0=c0, n=n, r=r: nc.vector.scalar_tensor_tensor(
                        out=xs[:, cp, c0:c0 + n], in0=ps[:, :n], scalar=ada_ap(l, r, gidx, cp), in1=xs[:, cp, c0:c0 + n],
                        op0=ALU.mult, op1=ALU.add), reads=[Bps, B_ada] + xb, writes=xb)

        def run_steps(steps, dist=2):
            slots = [None] * len(steps)

            def issue(i):
                if i < len(steps) and steps[i][0] is not None:
                    w, Bw = next_w()
                    steps[i][0](w, Bw)
                    slots[i] = (w, Bw)
            for i0 in range(dist):
                issue(i0)
            for i in range(len(steps)):
                issue(i + dist)
                w, Bw = slots[i] if slots[i] is not None else (None, None)
                steps[i][1](w, Bw)

        def conv_steps(l, j, b):
            tiles = ALL_TILES
            steps = []

            def mk_in(c):
                def load(w, Bw):
                    wv = w[:, 0:3072].rearrange("p (k n) -> p k n", n=384)
                    P.dma(pool, [(wv[:, :, s * 128:(s + 1) * 128],
                                  conv_w_in_d[j, :, s * 1024 + c * 128:s * 1024 + (c + 1) * 128].rearrange("(k p) n -> p k n", p=128))
                                 for s in range(3)], writes=[Bw])

                def comp(w, Bw):
                    wv = w[:, 0:3072].rearrange("p (k n) -> p k n", n=384)
                    slot = c % GRP
                    for (c0, n) in tiles:
                        pss = [next_ps() for _ in range(3)]
                        for s in range(3):
                            mm_group(pss[s][0][:, :n], [(wv[:, k, s * 128:(s + 1) * 128], hT[:, k, c0:c0 + n]) for k in range(KC)],
                                     reads=[Bw] + hT_bufs(c0, n), writes=[pss[s][1]])
                        (psb, Bpb), (psc, Bpc), (psh, Bph) = pss
                        gcs, Bg = next_tmp()
                        P.op(act, lambda gcs=gcs, psc=psc: nc.scalar.copy(gcs[:, :n], psc[:, :n]), reads=[Bpc], writes=[Bg])
                        u, Bu = next_tmp()
                        P.op(dve, lambda u=u, psh=psh, gcs=gcs: nc.vector.tensor_tensor(u[:, :n], psh[:, :n], gcs[:, :n], ALU.mult),
                             reads=[Bph, Bg], writes=[Bu])
                        cv, Bc = next_tmp()
                        P.op(act, lambda cv=cv, u=u: nc.scalar.activation(out=cv[:, :n], in_=u[:, :n], func=AF.Copy,
                                                                             scale=convw[:, (j * 3 + 1) * KC + c:(j * 3 + 1) * KC + c + 1]),
                             reads=[Bu, B_par], writes=[Bc])
                        wdt = 256 if c0 < TCX else 64
                        u3 = u[:, :n].rearrange("p (r w) -> p r w", w=wdt)
                        c3v = cv[:, :n].rearrange("p (r w) -> p r w", w=wdt)
                        P.op(dve, lambda u3=u3, c3v=c3v: nc.vector.scalar_tensor_tensor(
                            out=c3v[:, :, 1:wdt], in0=u3[:, :, 0:wdt - 1], scalar=convw[:, (j * 3 + 0) * KC + c:(j * 3 + 0) * KC + c + 1], in1=c3v[:, :, 1:wdt],
                            op0=ALU.mult, op1=ALU.add), reads=[Bu, Bc, B_par], writes=[Bc])
                        P.op(dve, lambda u3=u3, c3v=c3v: nc.vector.scalar_tensor_tensor(
                            out=c3v[:, :, 0:wdt - 1], in0=u3[:, :, 1:wdt], scalar=convw[:, (j * 3 + 2) * KC + c:(j * 3 + 2) * KC + c + 1], in1=c3v[:, :, 0:wdt - 1],
                            op0=ALU.mult, op1=ALU.add), reads=[Bu, Bc, B_par], writes=[Bc])
                        P.op(dve, lambda psb=psb, cv=cv: nc.vector.tensor_tensor(zb[:, slot, c0:c0 + n], psb[:, :n], cv[:, :n], ALU.mult),
                             reads=[Bpb, Bc], writes=[B_z[slot]])
                return (load, comp)

            def mk_out(g0, G):
                def load(w, Bw):
                    wv = w[:, 0:G * 1024].rearrange("p (k n) -> p k n", n=1024)
                    P.dma(pool, [(wv, conv_w_out_d[j, g0 * 128:(g0 + G) * 128, :].rearrange("(k p) n -> p k n", p=128))], writes=[Bw])

                def comp(w, Bw):
                    wv = w[:, 0:G * 1024].rearrange("p (k n) -> p k n", n=1024)
                    wout_compute(wv, Bw, G, tiles, l, b, 2)
                return (load, comp)
            for c in range(0, 6):
                steps.append(mk_in(c))
            steps.append(mk_out(0, 6))
            for c in range(6, 8):
                steps.append(mk_in(c))
            steps.append(mk_out(6, 2))
            return steps

        def ffn_steps(l, b, tiles):
            steps = []

            def mk_in(f):
                def load(w, Bw):
                    wv = w[:, 0:2048].rearrange("p (k n) -> p k n", n=256)
                    P.dma(pool, [(wv[:, :, s * 128:(s + 1) * 128],
                                  ffn_w_in_d[l, :, s * DFF + f * 128:s * DFF + (f + 1) * 128].rearrange("(k p) n -> p k n", p=128))
                                 for s in range(2)], writes=[Bw])

                def comp(w, Bw):
                    wv = w[:, 0:2048].rearrange("p (k n) -> p k n", n=256)
                    slot = f % GRP
                    for (c0, n) in tiles:
                        (psg, Bpg), (psu, Bpu) = next_ps(), next_ps()
                        mm_group(psg[:, :n], [(wv[:, k, 0:128], hT[:, k, c0:c0 + n]) for k in range(KC)],
                                 reads=[Bw] + hT_bufs(c0, n), writes=[Bpg])
                        mm_group(psu[:, :n], [(wv[:, k, 128:256], hT[:, k, c0:c0 + n]) for k in range(KC)],
                                 reads=[Bw] + hT_bufs(c0, n), writes=[Bpu])
                        sg, Bs = next_tmp()
                        P.op(act, lambda sg=sg, psg=psg: nc.scalar.activation(out=sg[:, :n], in_=psg[:, :n], func=AF.Silu),
                             reads=[Bpg], writes=[Bs])
                        P.op(dve, lambda sg=sg, psu=psu: nc.vector.tensor_tensor(zb[:, slot, c0:c0 + n], psu[:, :n], sg[:, :n], ALU.mult),
                             reads=[Bpu, Bs], writes=[B_z[slot]])
                return (load, comp)

            def mk_out(g0, G):
                def load(w, Bw):
                    wv = w[:, 0:G * 1024].rearrange("p (k n) -> p k n", n=1024)
                    P.dma(pool, [(wv, ffn_w_out_d[l, g0 * 128:(g0 + G) * 128, :].rearrange("(k p) n -> p k n", p=128))], writes=[Bw])

                def comp(w, Bw):
                    wv = w[:, 0:G * 1024].rearrange("p (k n) -> p k n", n=1024)
                    wout_compute(wv, Bw, G, tiles, l, b, 5)
                return (load, comp)
            for g0 in range(0, NF, GRP):
                G = min(GRP, NF - g0)
                for f in range(g0, g0 + G):
                    steps.append(mk_in(f))
                steps.append(mk_out(g0, G))
            return steps

        LNSC = float(np.log(128.0 ** -0.5))

        def gla_phase(l, j, b, last):
            st0, Bst0 = next_stage()
            st1, Bst1 = next_stage()
            for z, (st, Bst) in enumerate(((st0, Bst0), (st1, Bst1))):
                P.dma(sp, [(st[0:16, :], gla_wa2_d[j, z, :, :]), (st[16:17, :], gla_ba_d[j, z:z + 1, :])], writes=[Bst])
                P.op(dve, lambda st=st, z=z: nc.vector.tensor_copy(wa2[0:17, z, :], st[0:17, :]), reads=[Bst], writes=[B_wa])
            P.dma(pool, [(wa1[:, :, z * 16:(z + 1) * 16], gla_wa1_d[j, z, :, :].rearrange("(k p) r -> p k r", p=128)) for z in range(2)],
                  writes=[B_wa], sembuf=B_wa)
            P.dma(sp, [(gng[:], gla_ng_d[j, :, :])], writes=[B_gl], sembuf=B_gl)
            for z in range(2):
                P.op(dve, lambda z=z: nc.vector.memset(r1aug[z], 1.0), writes=[B_r1[z]])
            for (c0, n) in ALL_TILES:
                for z in range(2):
                    ps, Bps = next_ps()
                    mm_group(ps[0:16, :n], [(wa1[:, k, z * 16:(z + 1) * 16], hT[:, k, c0:c0 + n]) for k in range(KC)],
                             reads=[B_wa] + hT_bufs(c0, n), writes=[Bps])
                    P.op(act, lambda ps=ps, z=z, c0=c0, n=n: nc.scalar.copy(r1aug[z][0:16, c0:c0 + n], ps[0:16, :n]),
                         reads=[Bps], writes=[B_r1[z]])

            def head_steps(h):
                def load_in(w, Bw):
                    wv = w[:, 0:6144].rearrange("p (k n) -> p k n", n=768)
                    srcs = [(0, 128, h * 128), (128, 128, 512 + h * 128), (256, 256, 2048 + h * 256), (512, 256, 1024 + h * 256)]
                    P.dma(pool, [(wv[:, :, o:o + wd], gla_w_in_d[j, :, s:s + wd].rearrange("(k p) n -> p k n", p=128))
                                 for (o, wd, s) in srcs], writes=[Bw])

                def load_out(w, Bw):
                    wv = w[:, 0:2048].rearrange("p (k n) -> p k n", n=1024)
                    P.dma(pool, [(wv, gla_w_out_d[j, h * 256:(h + 1) * 256, :].rearrange("(k p) n -> p k n", p=128))], writes=[Bw])

                state = {}

                def comp_in(w, Bw):
                    state["w"] = (w, Bw)

                def comp_out(wo, Bwo):
                    w, Bw = state["w"]
                    wv = w[:, 0:6144].rearrange("p (k n) -> p k n", n=768)
                    wov = wo[:, 0:2048].rearrange("p (k n) -> p k n", n=1024)
                    P.dma(sp, [(brh[:], gla_br_d[j, :, h * 256:(h + 1) * 256])], writes=[B_gl], sembuf=B_gl)
                    P.op(dve, lambda: nc.vector.memset(Sst[:], 0.0), writes=[B_S])
                    order = [1, 0] + list(range(NCH - 1, 1, -1))
                    for idx, n in enumerate(order):
                        cs = slice(n * 128, (n + 1) * 128)
                        P.op(act, lambda n=n: nc.scalar.copy(sbprev[:, n, :], Sst[:]), reads=[B_S], writes=[B_sbp])
                        psk, Bpk = next_ps()
                        mm_group(psk[:, 0:128], [(hT[:, k, cs], wv[:, k, 128:256]) for k in range(KC)], reads=[Bw, B_hT[n]], writes=[Bpk])
                        psv_, Bpv = next_ps()
                        mm_group(psv_[:, 0:256], [(hT[:, k, cs], wv[:, k, 512:768]) for k in range(KC)], reads=[Bw, B_hT[n]], writes=[Bpv])
                        psl, Bpl = next_ps()
                        mm_group(psl[:, 0:128], [(r1aug[1][0:17, cs], wa2[0:17, 1, h * 128:(h + 1) * 128])], reads=[B_r1[1], B_wa], writes=[Bpl])
                        e1, Be1 = next_tmp()
                        P.op(act, lambda e1=e1, psl=psl: nc.scalar.activation(out=e1[:, 0:128], in_=psl[:, 0:128], func=AF.Exp, scale=-1.0),
                             reads=[Bpl], writes=[Be1])
                        P.op(act, lambda e1=e1: nc.scalar.activation(out=e1[:, 0:128], in_=e1[:, 0:128], func=AF.Ln, bias=1.0),
                             reads=[Be1], writes=[Be1])
                        P.op(act, lambda n=n, psv_=psv_: nc.scalar.copy(vtm[:, n, :], psv_[:, 0:256]), reads=[Bpv], writes=[B_vtm])
                        if idx == len(order) - 1:
                            break
                        psc, Bpc = next_ps()
                        mm_group(psc[:, 0:128], [(M4, e1[:, 0:128])], reads=[B_cst, Be1], writes=[Bpc])
                        mm_group(psc[:, 128:129], [(e1[:, 0:128], negcol)], reads=[B_cst, Be1], writes=[Bpc])
                        P.op(act, lambda psc=psc: nc.scalar.activation(out=ec_t[:], in_=psc[:, 0:128], func=AF.Exp), reads=[Bpc], writes=[B_ec])
                        P.op(act, lambda psc=psc: nc.scalar.activation(out=dec_t[:, 0:1], in_=psc[:, 128:129], func=AF.Exp), reads=[Bpc], writes=[B_dec])
                        kd, Bkd = kdec_t[idx % 2], B_kdec[idx % 2]
                        P.op(dve, lambda kd=kd, psk=psk: nc.vector.tensor_tensor(kd[:], psk[:, 0:128], ec_t[:], ALU.mult),
                             reads=[Bpk, B_ec], writes=[Bkd])
                        psu, Bpu = next_ps()
                        mm_group(psu[:, 0:256], [(kd[:], vtm[:, n, :])], reads=[Bkd, B_vtm], writes=[Bpu])
                        P.op(dve, lambda psu=psu: nc.vector.scalar_tensor_tensor(out=Sst[:], in0=Sst[:], scalar=dec_t[:, 0:1], in1=psu[:, 0:256],
                                                                                op0=ALU.mult, op1=ALU.add),
                             reads=[B_S, B_dec, Bpu], writes=[B_S])
                    P.op(dve, lambda: nc.vector.memset(Sst[:], 0.0), writes=[B_S])
                    for n in range(NCH):
                        cs = slice(n * 128, (n + 1) * 128)
                        outn = not (last and n < 2)
                        r = nb if n < 2 else b
                        sfb, Bsfb = Sbf[n % 2], B_Sbf[n % 2]
                        P.op(act, lambda sfb=sfb: nc.scalar.copy(sfb[:], Sst[:]), reads=[B_S], writes=[Bsfb])
                        pskr, Bpkr = next_ps()
                        ncol = 384 if outn else 128
                        mm_group(pskr[:, 0:ncol], [(hT[:, k, cs], wv[:, k, 128:128 + ncol]) for k in range(KC)], reads=[Bw, B_hT[n]], writes=[Bpkr])
                        psl, Bpl = next_ps()
                        nz = 2 if outn else 1
                        for z in range(nz):
                            mm_group(psl[:, z * 128:(z + 1) * 128], [(r1aug[z][0:17, cs], wa2[0:17, z, h * 128:(h + 1) * 128])],
                                     reads=[B_r1[z], B_wa], writes=[Bpl])
                        e1, Be1 = next_tmp()
                        P.op(act, lambda e1=e1, psl=psl, nz=nz: nc.scalar.activation(out=e1[:, 0:128 * nz], in_=psl[:, 0:128 * nz], func=AF.Exp, scale=-1.0),
                             reads=[Bpl], writes=[Be1])
                        P.op(act, lambda e1=e1, nz=nz: nc.scalar.activation(out=e1[:, 0:128 * nz], in_=e1[:, 0:128 * nz], func=AF.Ln, bias=1.0),
                             reads=[Be1], writes=[Be1])
                        psc, Bpc = next_ps()
                        mm_group(psc[:, 0:128], [(M2, e1[:, 0:128])], reads=[B_cst, Be1], writes=[Bpc])
                        mm_group(psc[:, 128:256], [(e1[:, 0:128], M1)], reads=[B_cst, Be1], writes=[Bpc])
                        if outn:
                            mm_group(psc[:, 256:384], [(e1[:, 128:256], M3)], reads=[B_cst, Be1], writes=[Bpc])
                        P.op(act, lambda psc=psc: nc.scalar.activation(out=ec_t[:], in_=psc[:, 0:128], func=AF.Exp), reads=[Bpc], writes=[B_ec])
                        P.op(act, lambda psc=psc: nc.scalar.activation(out=dec_t[:, 1:2], in_=psc[:, 255:256], func=AF.Exp), reads=[Bpc], writes=[B_dec])
                        kd, Bkd = kdec_t[n % 2], B_kdec[n % 2]
                        P.op(dve, lambda kd=kd, pskr=pskr: nc.vector.tensor_tensor(kd[:], pskr[:, 0:128], ec_t[:], ALU.mult),
                             reads=[Bpkr, B_ec], writes=[Bkd])
                        if outn:
                            P.op(act, lambda psc=psc: nc.scalar.activation(out=Eq_t[:], in_=psc[:, 128:384], func=AF.Exp, bias=LNSC),
                                 reads=[Bpc], writes=[B_Eq])
                            P.op(act, lambda psc=psc: nc.scalar.activation(out=Ek_t[:], in_=psc[:, 128:384], func=AF.Exp, scale=-1.0),
                                 reads=[Bpc], writes=[B_Ek])
                            psqk, Bpqk = next_ps()
                            mm_group(psqk[:, 0:128], [(wv[:, k, 0:128], hT[:, k, cs]) for k in range(KC)], reads=[Bw, B_hT[n]], writes=[Bpqk])
                            mm_group(psqk[:, 128:256], [(wv[:, k, 128:256], hT[:, k, cs]) for k in range(KC)], reads=[Bw, B_hT[n]], writes=[Bpqk])
                            for z in range(2):
                                P.op(dve, lambda z=z, psqk=psqk: nc.vector.tensor_tensor(qdec_t[:, z, :], psqk[:, 0:128], Eq_t[:, z * 128:(z + 1) * 128], ALU.mult),
                                     reads=[Bpqk, B_Eq], writes=[B_qdec])
                                P.op(dve, lambda z=z, psqk=psqk: nc.vector.tensor_tensor(kinv_t[:, z, :], psqk[:, 128:256], Ek_t[:, z * 128:(z + 1) * 128], ALU.mult),
                                     reads=[Bpqk, B_Ek], writes=[B_kinv])
                            pss, Bpss = next_ps()
                            for z in range(2):
                                mm_group(pss[:, z * 128:(z + 1) * 128], [(kinv_t[:, z, :], qdec_t[:, z, :])], reads=[B_kinv, B_qdec], writes=[Bpss])
                            P.op(dve, lambda pss=pss: nc.vector.tensor_tensor(sm_t[:].rearrange("p (z c) -> p z c", z=2),
                                                                              pss[:, 0:256].rearrange("p (z c) -> p z c", z=2), mask2, ALU.mult),
                                 reads=[Bpss, B_cst], writes=[B_sm])
                            pso, Bpo = next_ps()
                            mm_group(pso[:, 0:256], [(sm_t[:, 0:128], vtm[:, n, :]), (qdec_t[:, 0, :], sfb[:]),
                                                     (sm_t[:, 128:256], vtm[:, n, :]), (qdec_t[:, 1, :], sbprev[:, n, :])],
                                     reads=[B_sm, B_vtm, B_qdec, Bsfb, B_sbp], writes=[Bpo])
                        if n < NCH - 1:
                            psu, Bpu = next_ps()
                            mm_group(psu[:, 0:256], [(kd[:], vtm[:, n, :])], reads=[Bkd, B_vtm], writes=[Bpu])
                            P.op(dve, lambda psu=psu: nc.vector.scalar_tensor_tensor(out=Sst[:], in0=Sst[:], scalar=dec_t[:, 1:2], in1=psu[:, 0:256],
                                                                                    op0=ALU.mult, op1=ALU.add),
                                 reads=[B_S, B_dec, Bpu], writes=[B_S])
                        if not outn:
                            continue
                        junk, Bj = next_tmp()
                        P.op(act, lambda junk=junk, pso=pso: nc.scalar.activation(out=junk[:, 0:256], in_=pso[:, 0:256], func=AF.Square, accum_out=dec_t[:, 2:3]),
                             reads=[Bpo], writes=[Bj, B_dec])
                        P.op(act, lambda: nc.scalar.activation(out=dec_t[:, 3:4], in_=dec_t[:, 2:3], func=AF.Ln, bias=EPS, scale=1.0 / 256),
                             reads=[B_dec], writes=[B_dec])
                        P.op(act, lambda: nc.scalar.activation(out=dec_t[:, 3:4], in_=dec_t[:, 3:4], func=AF.Exp, scale=-0.5),
                             reads=[B_dec], writes=[B_dec])
                        rr, Brr = next_tmp()
                        P.op(dve, lambda rr=rr, pskr=pskr: nc.vector.tensor_tensor(rr[:, 0:256], pskr[:, 128:384], brh[:], ALU.add),
                             reads=[Bpkr, B_gl], writes=[Brr])
                        P.op(act, lambda rr=rr: nc.scalar.activation(out=rr[:, 256:512], in_=rr[:, 0:256], func=AF.Silu), reads=[Brr], writes=[Brr])
                        t1, Bt1 = next_tmp()
                        P.op(dve, lambda t1=t1, pso=pso: nc.vector.scalar_tensor_tensor(out=t1[:, 0:256], in0=pso[:, 0:256], scalar=dec_t[:, 3:4], in1=gng[:],
                                                                                     op0=ALU.mult, op1=ALU.mult),
                             reads=[Bpo, B_dec, B_gl], writes=[Bt1])
                        P.op(dve, lambda t1=t1, rr=rr: nc.vector.tensor_tensor(og_t[:], t1[:, 0:256], rr[:, 256:512], ALU.mult),
                             reads=[Bt1, Brr], writes=[B_og])

                        def tr():
                            nc.tensor.transpose(ptb[:, 0:128], og_t[:, 0:128], ident_b[:])
                            return nc.tensor.transpose(ptb[:, 128:256], og_t[:, 128:256], ident_b[:])
                        P.op(pe, tr, reads=[B_og, B_idb], writes=[B_ptb])
                        P.op(act, lambda: nc.scalar.copy(ogT_t[:].rearrange("p j t -> p (j t)"), ptb[:, 0:256]), reads=[B_ptb], writes=[B_ogT])
                        for half in range(2):
                            psy, Bpy = next_ps()
                            for q in range(4):
                                cp = half * 4 + q
                                mm_group(psy[:, q * 128:(q + 1) * 128], [(wov[:, jj, cp * 128:(cp + 1) * 128], ogT_t[:, jj, :]) for jj in range(2)],
                                         reads=[Bwo, B_ogT], writes=[Bpy])
                            for q in range(4):
                                cp = half * 4 + q
                                P.op(dve, lambda psy=psy, q=q, cp=cp, r=r, cs=cs: nc.vector.scalar_tensor_tensor(
                                    out=xs[:, cp, cs], in0=psy[:, q * 128:(q + 1) * 128], scalar=ada_ap(l, r, 2, cp), in1=xs[:, cp, cs],
                                    op0=ALU.mult, op1=ALU.add), reads=[Bpy, B_ada, B_xs[cp][n]], writes=[B_xs[cp][n]])
                return [(load_in, comp_in), (load_out, comp_out)]
            steps = []
            for h in range(4):
                steps += head_steps(h)
            return steps

        def final_phase(b):
            for (c0, n) in LAT_TILES:
                rstd_tile(c0, n)
                for k in range(KC):
                    t, Bt = next_tmp()
                    P.op(dve, lambda t=t, k=k: nc.vector.tensor_tensor(t[:, :n], xs[:, k, c0:c0 + n], rs[:, :n], ALU.mult),
                         reads=xs_bufs(k, c0, n) + [B_rs], writes=[Bt])
                    P.op(act, lambda t=t, k=k: nc.scalar.activation(out=xs[:, k, c0:c0 + n], in_=t[:, :n], func=AF.Copy, scale=fing[:, k:k + 1]),
                         reads=[Bt, B_par], writes=xs_bufs(k, c0, n))

        if "ada" in phases:
            run_steps(ada_steps())
        for b in range(nb):
            load_x(b)
            for l in layers:
                last = (l == DEPTH - 1)
                kind = l % 2
                j = l // 2
                if "norm1" in phases:
                    norm_phase(l, 0, b, ALL_TILES)
                if "mixer" in phases:
                    if kind == 0:
                        run_steps(conv_steps(l, j, b))
                    else:
                        run_steps(gla_phase(l, j, b, last), dist=1)
                tiles = LAT_TILES if last else ALL_TILES
                if "norm2" in phases:
                    norm_phase(l, 1, b, tiles)
                if "ffn" in phases:
                    run_steps(ffn_steps(l, b, tiles))
            if final:
                final_phase(b)
            store_x(b, range(2, NCH), xo_d, 2)
            if want_ctx_out:
                store_x(b, range(0, 2), ctxo_d, 0)
        for Bst in B_stage:
            if Bst.dsem is not None:
                sp.e.wait_ge(Bst.dsem[0], Bst.dsem[1])
    return nc


def _consts():
    s = np.arange(128)[:, None]
    t = np.arange(128)[None, :]
    c = np.zeros((128, 8, 128), np.float32)
    c[:, 0, :] = np.eye(128, dtype=np.float32)
    c[:, 1, :] = np.where(s <= t, -1.0 / 16, 0.0)
    c[:, 2, :] = np.where(s > t, -1.0 / 16, 0.0)
    c[:, 3, :] = np.where(s >= t, -1.0 / 16, 0.0)
    c[:, 4, :] = np.where(s < t, -1.0 / 16, 0.0)
    c[:, 5, :] = np.where(s <= t, 1.0, 0.0)
    c[:, 6, :] = np.where(s >= t, 1.0, 0.0)
    c[:, 7, :] = -1.0 / 16
    return c


def _fm(v):
    v = np.asarray(v, np.float32)
    lead = v.shape[:-1]
    a = v.reshape(lead + (KC, 128))
    a = np.moveaxis(a, -1, 0)
    return np.ascontiguousarray(a)


def make_in_maps(x, c, ctx, c_ctx, ada_w, ada_b, norm1_g, norm2_g, conv_w_in, conv_w, conv_w_out,
                 gla_w_in, gla_b_r, gla_w_a1, gla_w_a2, gla_b_a, gla_norm_g, gla_w_out,
                 ffn_w_in, ffn_w_out, final_g, ncores=NCORES):
    f = lambda a: np.ascontiguousarray(np.asarray(a, np.float32))
    shared = {
        "ada_w": f(ada_w),
        "ada_b": np.ascontiguousarray(np.moveaxis(f(ada_b).reshape(DEPTH, 48, 128), -1, 0)),
        "n1g": _fm(norm1_g), "n2g": _fm(norm2_g),
        "conv_w_in": f(conv_w_in), "convw": _fm(conv_w), "conv_w_out": f(conv_w_out),
        "gla_w_in": f(gla_w_in),
        "gla_br": np.ascontiguousarray(np.broadcast_to(f(gla_b_r)[:, None, :], (2, 128, D))),
        "gla_wa1": f(gla_w_a1), "gla_wa2": f(gla_w_a2), "gla_ba": f(gla_b_a),
        "gla_ng": np.ascontiguousarray(np.broadcast_to(f(gla_norm_g)[:, None, :], (2, 128, 256))),
        "gla_w_out": f(gla_w_out), "ffn_w_in": f(ffn_w_in), "ffn_w_out": f(ffn_w_out),
        "fing": _fm(final_g), "cst": _consts(),
    }
    x = f(x)
    c = f(c)
    ctx = f(ctx)
    c_ctx = f(c_ctx)
    maps = []
    nb = 16 // ncores
    for i in range(ncores):
        c3 = np.concatenate([c[nb * i:nb * (i + 1)], c_ctx[None, :]], 0)
        m = dict(shared)
        m["x"] = np.ascontiguousarray(x[nb * i:nb * (i + 1)])
        m["ctx"] = np.ascontiguousarray(ctx[nb * i:nb * (i + 1)])
        m["c3"] = np.ascontiguousarray(np.moveaxis(c3.reshape(nb + 1, KC, 128), (0, 1, 2), (2, 1, 0)))
        maps.append(m)
    return maps


_NC_CACHE = {}


def kernel(**inputs):
    ncores = CORES_PER_LAUNCH
    maps = make_in_maps(ncores=ncores, **inputs)
    if "full" not in _NC_CACHE:
        _NC_CACHE["full"] = build_program(nb=16 // ncores)
    nc = _NC_CACHE["full"]
    res = run_bass_kernel_spmd(nc, maps, core_ids=list(range(ncores)))
    return np.concatenate([np.asarray(r["xo"], np.float32) for r in res.results], axis=0)
```

```python
import numpy as np
from contextlib import ExitStack
import concourse.bass as bass
import concourse.mybir as mybir
from concourse.bass_utils import run_bass_kernel_spmd

F32 = mybir.dt.float32
BF16 = mybir.dt.bfloat16
AF = mybir.ActivationFunctionType
ALU = mybir.AluOpType

D = 1024
KC = 8
TCX = 256
TL = 2048
T = TCX + TL
NCH = T // 128
DFF = 2816
NF = DFF // 128
DEPTH = 4
EPS = 1e-6
NCORES = 8
CORES_PER_LAUNCH = 4
GRP = 6

ALL_TILES = [(0, 256)] + [(256 + 512 * i, 512) for i in range(4)]
LAT_TILES = ALL_TILES[1:]


class Buf:
    _reg = []

    def __init__(self, name, region=None):
        self.name = name
        self.w = None
        self.r = {}
        self.ov = []
        self.region = region
        self.dsem = None
        if region is not None:
            for o in Buf._reg:
                if o.region[0] == region[0] and o.region[1] < region[2] and region[1] < o.region[2]:
                    o.ov.append(self)
                    self.ov.append(o)
            Buf._reg.append(self)


class Q:
    def __init__(self, name, eng, sem):
        self.name, self.e, self.sem, self.cnt, self.waited = name, eng, sem, 0, {}


class Prog:
    def __init__(self, nc, es):
        self.nc = nc
        self.es = es
        self.nsem = 0
        mk = lambda n, e: Q(n, e, self.sem(n))
        self.pe = mk("pe", nc.tensor)
        self.act = mk("act", nc.scalar)
        self.dve = mk("dve", nc.vector)
        self.pool = mk("pool", nc.gpsimd)
        self.sp = mk("sp", nc.sync)

    def sem(self, name):
        self.nsem += 1
        return self.es.enter_context(self.nc.semaphore(f"s_{name}_{self.nsem}"))

    def sb(self, name, shape, dt):
        return self.es.enter_context(self.nc.sbuf_tensor("t_" + name, shape, dt))

    def _deps(self, reads, writes):
        d = {}

        def add(t):
            if t is not None:
                if d.get(t[0], 0) < t[1]:
                    d[t[0]] = t[1]

        for b in reads:
            add(b.w)
            for o in b.ov:
                add(o.w)
        for b in writes:
            for bb in [b] + b.ov:
                add(bb.w)
                for s, v in bb.r.items():
                    add((s, v))
        return d

    def _wait(self, q, d):
        for s, v in d.items():
            if q.name == "pe" and s is q.sem:
                continue
            if q.waited.get(s, 0) < v:
                q.e.wait_ge(s, v)
                q.waited[s] = v

    def op(self, q, fn, reads=(), writes=()):
        self._wait(q, self._deps(reads, writes))
        ins = fn()
        q.cnt += 1
        ins.then_inc(q.sem, 1)
        tag = (q.sem, q.cnt)
        for b in writes:
            b.w = tag
            b.r = {}
        for b in reads:
            b.r[q.sem] = q.cnt

    def dma(self, q, pairs, reads=(), writes=(), sembuf=None):
        sb = sembuf if sembuf is not None else (writes[0] if writes else reads[0])
        if sb.dsem is None:
            sb.dsem = [self.sem("d" + sb.name), 0]
        self._wait(q, self._deps(reads, writes))
        for o, i in pairs:
            q.e.dma_start(out=o, in_=i).then_inc(sb.dsem[0], 16)
            sb.dsem[1] += 16
        tag = (sb.dsem[0], sb.dsem[1])
        for b in writes:
            b.w = tag
            b.r = {}
        for b in reads:
            b.r[tag[0]] = tag[1]
        return tag


def build_program(layers=(0, 1, 2, 3), first=True, final=True, phases=("ada", "norm1", "mixer", "norm2", "ffn"), nb=2):
    Buf._reg = []
    NR = nb + 1
    nc = bass.Bass("TRN2", target_bir_lowering=False)

    def din(name, shape):
        return nc.dram_tensor(name, list(shape), F32, kind="ExternalInput").ap()

    x_d = din("x", (nb, TL, D))
    ctx_d = din("ctx", (nb, TCX, D))
    c3_d = din("c3", (128, KC, NR))
    ada_w_d = din("ada_w", (DEPTH, D, 6 * D))
    ada_b_d = din("ada_b", (128, DEPTH, 48))
    n1g_d = din("n1g", (128, DEPTH, KC))
    n2g_d = din("n2g", (128, DEPTH, KC))
    conv_w_in_d = din("conv_w_in", (2, D, 3 * D))
    convw_d = din("convw", (128, 2, 3, KC))
    conv_w_out_d = din("conv_w_out", (2, D, D))
    gla_w_in_d = din("gla_w_in", (2, D, 3 * D))
    gla_br_d = din("gla_br", (2, 128, D))
    gla_wa1_d = din("gla_wa1", (2, 2, D, 16))
    gla_wa2_d = din("gla_wa2", (2, 2, 16, 512))
    gla_ba_d = din("gla_ba", (2, 2, 512))
    gla_ng_d = din("gla_ng", (2, 128, 256))
    gla_w_out_d = din("gla_w_out", (2, D, D))
    ffn_w_in_d = din("ffn_w_in", (DEPTH, D, 2 * DFF))
    ffn_w_out_d = din("ffn_w_out", (DEPTH, DFF, D))
    fing_d = din("fing", (128, KC))
    cst_d = din("cst", (128, 8, 128))
    xo_d = nc.dram_tensor("xo", [nb, TL, D], F32, kind="ExternalOutput").ap()
    want_ctx_out = not final
    if want_ctx_out:
        ctxo_d = nc.dram_tensor("ctxo", [nb, TCX, D], F32, kind="ExternalOutput").ap()

    with ExitStack() as es:
        P = Prog(nc, es)
        pe, act, dve, pool, sp = P.pe, P.act, P.dve, P.pool, P.sp

        xs = P.sb("xs", [128, KC, T], F32)
        hT = P.sb("hT", [128, KC, T], BF16)
        zflat = P.sb("zb", [128, GRP * T], BF16)
        zb = zflat[:, :].rearrange("p (k t) -> p k t", t=T)
        Wt = [P.sb(f"W{i}", [128, 6144], BF16) for i in range(3)]
        stage = [P.sb(f"stage{i}", [128, 512], F32) for i in range(2)]
        tmpF = [P.sb(f"tmpF{i}", [128, 512], F32) for i in range(3)]
        sqb = [P.sb(f"sq{i}", [128, 512], BF16) for i in range(2)]
        rs = P.sb("rs", [128, 512], F32)
        cst = P.sb("cst", [128, 8, 128], F32)
        ident_b = P.sb("ident_b", [128, 128], BF16)
        ones_b = P.sb("ones_b", [128, 128], BF16)
        ada_b_t = P.sb("ada_b_t", [128, DEPTH, 48], F32)
        ada_sb = P.sb("ada_sb", [128, DEPTH * NR * 48], F32)
        n1g = P.sb("n1g", [128, DEPTH, KC], F32)
        n2g = P.sb("n2g", [128, DEPTH, KC], F32)
        convw = P.sb("convw", [128, 2 * 3 * KC], F32)
        fing = P.sb("fing", [128, KC], F32)
        c3f = P.sb("c3f", [128, KC, NR], F32)
        c3b = P.sb("c3b", [128, KC, NR], BF16)
        brh = P.sb("brh", [128, 256], F32)
        gng = P.sb("gng", [128, 256], F32)
        wa1 = P.sb("wa1", [128, KC, 32], BF16)
        wa2 = P.sb("wa2", [17, 2, 512], BF16)
        ec_t = P.sb("ec_t", [128, 128], F32)
        Eq_t = P.sb("Eq_t", [128, 256], F32)
        Ek_t = P.sb("Ek_t", [128, 256], F32)
        kdec_t = [P.sb(f"kdec{i}", [128, 128], BF16) for i in range(2)]
        qdec_t = P.sb("qdec", [128, 2, 128], BF16)
        kinv_t = P.sb("kinv", [128, 2, 128], BF16)
        sm_t = P.sb("sm", [128, 256], BF16)
        Sst = P.sb("Sst", [128, 256], F32)
        Sbf = [P.sb(f"Sbf{i}", [128, 256], BF16) for i in range(2)]
        dec_t = P.sb("dec_t", [128, 4], F32)
        og_t = P.sb("og_t", [128, 256], BF16)
        ogT_t = P.sb("ogT_t", [128, 2, 128], BF16)

        NPS = 7
        psf = [es.enter_context(nc.psum_tensor(f"ps{i}", [128, 512], F32)) for i in range(NPS)]
        ptb = es.enter_context(nc.psum_tensor("ptb", [128, 1024], BF16))

        B_xs = [[Buf(f"xs{k}_{n}") for n in range(NCH)] for k in range(KC)]
        B_hT = [Buf(f"hT{n}") for n in range(NCH)]
        B_z = [Buf(f"z{i}", ("z", i * T, (i + 1) * T)) for i in range(GRP)]
        B_W = [Buf(f"W{i}") for i in range(3)]
        B_stage = [Buf(f"stage{i}") for i in range(2)]
        B_tmp = [Buf(f"tmpF{i}") for i in range(3)]
        B_sq = [Buf(f"sq{i}") for i in range(2)]
        B_rs = Buf("rs")
        B_ps = [Buf(f"ps{i}") for i in range(NPS)]
        B_ptb = Buf("ptb")
        B_cst = Buf("cst")
        B_par = Buf("params")
        B_ada = Buf("ada")
        B_c3b = Buf("c3b")
        B_gl = Buf("glasmall")
        B_wa = Buf("wa")
        B_ec, B_Eq, B_Ek = Buf("ec"), Buf("Eq"), Buf("Ek")
        B_kdec = [Buf("kdec0"), Buf("kdec1")]
        B_qdec, B_kinv, B_sm = Buf("qdec"), Buf("kinv"), Buf("sm")
        B_S = Buf("Sst")
        B_Sbf = [Buf("Sbf0"), Buf("Sbf1")]
        B_dec = Buf("dec")
        B_og, B_ogT = Buf("og"), Buf("ogT")
        SB0, V0, R1F0, R1B0 = 0, 4608, 9216, 11520
        B_sbp = Buf("sbprev", ("z", SB0, V0))
        B_vtm = Buf("vtm", ("z", V0, R1F0))
        B_r1 = [Buf("r1f", ("z", R1F0, R1B0)), Buf("r1b", ("z", R1B0, R1B0 + T))]
        sbprev = zflat[:, SB0:V0].rearrange("p (n e) -> p n e", e=256)
        vtm = zflat[:, V0:R1F0].rearrange("p (n e) -> p n e", e=256)
        r1aug = [zflat[0:17, R1F0:R1B0], zflat[0:17, R1B0:R1B0 + T]]

        cnt = {"ps": 0, "tmp": 0, "sq": 0, "w": 0, "st": 0}

        def next_ps():
            i = cnt["ps"] % NPS
            cnt["ps"] += 1
            return psf[i], B_ps[i]

        def next_tmp():
            i = cnt["tmp"] % 3
            cnt["tmp"] += 1
            return tmpF[i], B_tmp[i]

        def next_sq():
            i = cnt["sq"] % 2
            cnt["sq"] += 1
            return sqb[i], B_sq[i]

        def next_w():
            i = cnt["w"] % 3
            cnt["w"] += 1
            return Wt[i], B_W[i]

        def next_stage():
            i = cnt["st"] % 2
            cnt["st"] += 1
            return stage[i], B_stage[i]

        def chunks_of(c0, n):
            return range(c0 // 128, (c0 + n) // 128)

        def xs_bufs(k, c0, n):
            return [B_xs[k][c] for c in chunks_of(c0, n)]

        def hT_bufs(c0, n):
            return [B_hT[c] for c in chunks_of(c0, n)]

        def mm_group(out_ap, pairs, reads, writes):
            def fn():
                ins = None
                for i, (l, r) in enumerate(pairs):
                    ins = nc.tensor.matmul(out_ap, l, r, start=(i == 0), stop=(i == len(pairs) - 1))
                return ins
            P.op(pe, fn, reads=reads, writes=writes)

        P.dma(sp, [(cst[:], cst_d[:, :, :])], writes=[B_cst])
        P.dma(sp, [(ada_b_t[:], ada_b_d[:, :, :]), (n1g[:], n1g_d[:, :, :]), (n2g[:], n2g_d[:, :, :]),
                   (convw[:], convw_d[:, :, :, :].rearrange("p j t k -> p (j t k)")), (fing[:], fing_d[:, :]), (c3f[:], c3_d[:, :, :])], writes=[B_par])
        ident_f = cst[:, 0, :]
        M1, M2, M3, M4 = cst[:, 1, :], cst[:, 2, :], cst[:, 3, :], cst[:, 4, :]
        mask2 = cst[:, 5:7, :]
        negcol = cst[:, 7, 0:1]
        B_idb = Buf("identb")
        P.op(dve, lambda: nc.vector.tensor_copy(ident_b[:], cst[:, 0, :]), reads=[B_cst], writes=[B_idb])
        P.op(dve, lambda: nc.vector.memset(ones_b[:], 1.0), writes=[B_idb])
        P.op(act, lambda: nc.scalar.activation(out=c3b[:], in_=c3f[:], func=AF.Silu), reads=[B_par], writes=[B_c3b])

        def ada_steps():
            steps = []
            holder = {}
            for l in layers:
                for blk in range(8):
                    def load(w, Bw, l=l, blk=blk):
                        wv = w[:, 0:6144].rearrange("p (k n) -> p k n", n=768)
                        P.dma(pool, [(wv, ada_w_d[l, :, blk * 768:(blk + 1) * 768].rearrange("(k p) n -> p k n", p=128))],
                              writes=[Bw])

                    def comp(w, Bw, l=l, blk=blk):
                        wv = w[:, 0:6144].rearrange("p (k n) -> p k n", n=768)
                        if blk == 0:
                            holder["ps"] = next_ps()
                        psA, BpsA = holder["ps"]
                        for jj in range(6):
                            j = blk * 6 + jj
                            mm_group(psA[:, j * NR:(j + 1) * NR],
                                     [(wv[:, k, jj * 128:(jj + 1) * 128], c3b[:, k, :]) for k in range(KC)],
                                     reads=[Bw, B_c3b], writes=[BpsA])
                        if blk < 7:
                            return
                        psv = psA[:, 0:48 * NR].rearrange("p (j r) -> p j r", r=NR)
                        for r in range(NR):
                            P.op(dve, lambda r=r: nc.vector.tensor_tensor(ada_sb[:, (l * NR + r) * 48:(l * NR + r + 1) * 48], psv[:, :, r], ada_b_t[:, l, :], ALU.add),
                                 reads=[BpsA, B_par], writes=[B_ada])
                        for r in range(NR):
                            for which, ng in ((0, n1g), (1, n2g)):
                                o0 = (l * NR + r) * 48 + 8 + 24 * which
                                sc = ada_sb[:, o0:o0 + 8]
                                P.op(dve, lambda sc=sc, ng=ng: nc.vector.scalar_tensor_tensor(
                                    out=sc, in0=sc, scalar=1.0, in1=ng[:, l, :], op0=ALU.add, op1=ALU.mult),
                                    reads=[B_ada, B_par], writes=[B_ada])
                    steps.append((load, comp))
            return steps

        def ada_ap(l, r, idx, k):
            o = (l * NR + r) * 48 + idx * 8 + k
            return ada_sb[:, o:o + 1]

        def load_x(b):
            for n in range(NCH):
                for half in range(2):
                    st, Bst = next_stage()
                    if n < 2:
                        src = ctx_d[b, n * 128:(n + 1) * 128, half * 512:(half + 1) * 512]
                    else:
                        src = x_d[b, (n - 2) * 128:(n - 1) * 128, half * 512:(half + 1) * 512]
                    P.dma(sp, [(st[:], src)], writes=[Bst])
                    ps, Bps = next_ps()

                    def fn(ps=ps, st=st):
                        ins = None
                        for q in range(4):
                            ins = nc.tensor.transpose(ps[:, q * 128:(q + 1) * 128], st[:, q * 128:(q + 1) * 128], ident_f)
                        return ins
                    P.op(pe, fn, reads=[Bst, B_cst], writes=[Bps])
                    dst = xs[:, half * 4:(half + 1) * 4, n * 128:(n + 1) * 128]
                    src_ps = ps[:, :].rearrange("p (q t) -> p q t", t=128)
                    wb = [B_xs[k][n] for k in range(half * 4, half * 4 + 4)]
                    if half == 0:
                        P.op(act, lambda dst=dst, s=src_ps: nc.scalar.copy(dst, s), reads=[Bps], writes=wb)
                    else:
                        P.op(dve, lambda dst=dst, s=src_ps: nc.vector.tensor_copy(dst, s), reads=[Bps], writes=wb)

        def store_x(b, chunks, dst_d, dst_off):
            for n in chunks:
                for half in range(2):
                    ps, Bps = next_ps()

                    def fn(ps=ps, n=n, half=half):
                        ins = None
                        for q in range(4):
                            ins = nc.tensor.transpose(ps[:, q * 128:(q + 1) * 128], xs[:, half * 4 + q, n * 128:(n + 1) * 128], ident_f)
                        return ins
                    P.op(pe, fn, reads=[B_xs[k][n] for k in range(half * 4, half * 4 + 4)] + [B_cst], writes=[Bps])
                    st, Bst = next_stage()
                    if half == 0:
                        P.op(act, lambda st=st, ps=ps: nc.scalar.copy(st[:], ps[:, :]), reads=[Bps], writes=[Bst])
                    else:
                        P.op(dve, lambda st=st, ps=ps: nc.vector.tensor_copy(st[:], ps[:, :]), reads=[Bps], writes=[Bst])
                    row = (n - dst_off) * 128
                    P.dma(sp, [(dst_d[b, row:row + 128, half * 512:(half + 1) * 512], st[:])], reads=[Bst], sembuf=Bst)

        def rstd_tile(c0, n):
            ps, Bps = next_ps()
            for k in range(KC):
                sq, Bsq = next_sq()
                xin = xs[:, k, c0:c0 + n]
                if k % 2 == 0:
                    P.op(act, lambda sq=sq, xin=xin: nc.scalar.activation(out=sq[:, :n], in_=xin, func=AF.Square),
                         reads=xs_bufs(k, c0, n), writes=[Bsq])
                else:
                    P.op(dve, lambda sq=sq, xin=xin: nc.vector.tensor_tensor(sq[:, :n], xin, xin, ALU.mult),
                         reads=xs_bufs(k, c0, n), writes=[Bsq])
                P.op(pe, lambda sq=sq, ps=ps, k=k: nc.tensor.matmul(ps[:, :n], ones_b[:], sq[:, :n], start=(k == 0), stop=(k == KC - 1)),
                     reads=[Bsq, B_idb], writes=[Bps])
            P.op(act, lambda ps=ps: nc.scalar.activation(out=rs[:, :n], in_=ps[:, :n], func=AF.Ln, bias=EPS, scale=1.0 / D),
                 reads=[Bps], writes=[B_rs])
            P.op(act, lambda: nc.scalar.activation(out=rs[:, :n], in_=rs[:, :n], func=AF.Exp, scale=-0.5),
                 reads=[B_rs], writes=[B_rs])

        def norm_phase(l, which, b, tiles):
            import os
            dbg = os.environ.get("NORMDBG", "")
            for (c0, n) in tiles:
                r = nb if c0 < TCX else b
                rstd_tile(c0, n)
                if dbg == "rstd":
                    continue
                for k in range(KC):
                    t, Bt = next_tmp()
                    P.op(dve, lambda t=t, k=k: nc.vector.tensor_tensor(t[:, :n], xs[:, k, c0:c0 + n], rs[:, :n], ALU.mult),
                         reads=xs_bufs(k, c0, n) + [B_rs], writes=[Bt])
                    P.op(act, lambda t=t, k=k, r=r: nc.scalar.activation(
                        out=hT[:, k, c0:c0 + n], in_=t[:, :n], func=AF.Identity,
                        bias=ada_ap(l, r, 3 * which, k), scale=ada_ap(l, r, 3 * which + 1, k)),
                        reads=[Bt, B_ada], writes=hT_bufs(c0, n))

        def wout_compute(wv, Bw, G, tiles, l, b, gidx):
            for cp in range(KC):
                for (c0, n) in tiles:
                    r = nb if c0 < TCX else b
                    ps, Bps = next_ps()
                    mm_group(ps[:, :n], [(wv[:, kk, cp * 128:(cp + 1) * 128], zb[:, kk, c0:c0 + n]) for kk in range(G)],
                             reads=[Bw] + B_z[:G], writes=[Bps])
                    xb = xs_bufs(cp, c0, n)
                    P.op(dve, lambda ps=ps, cp=cp, c0=c0, n=n, r=r: nc.vector.scalar_tensor_tensor(
                        out=xs[:, cp, c0:c0 + n], in0=ps[:, :n], scalar=ada_ap(l, r, gidx, cp), in1=xs[:, cp, c0:c0 + n],
                        op0=ALU.mult, op1=ALU.add), reads=[Bps, B_ada] + xb, writes=xb)

        def run_steps(steps, dist=2):
            slots = [None] * len(steps)

            def issue(i):
                if i < len(steps) and steps[i][0] is not None:
                    w, Bw = next_w()
                    steps[i][0](w, Bw)
                    slots[i] = (w, Bw)
            for i0 in range(dist):
                issue(i0)
            for i in range(len(steps)):
                issue(i + dist)
                w, Bw = slots[i] if slots[i] is not None else (None, None)
                steps[i][1](w, Bw)

        def conv_steps(l, j, b):
            tiles = ALL_TILES
            steps = []

            def mk_in(c):
                def load(w, Bw):
                    wv = w[:, 0:3072].rearrange("p (k n) -> p k n", n=384)
                    P.dma(pool, [(wv[:, :, s * 128:(s + 1) * 128],
                                  conv_w_in_d[j, :, s * 1024 + c * 128:s * 1024 + (c + 1) * 128].rearrange("(k p) n -> p k n", p=128))
                                 for s in range(3)], writes=[Bw])

                def comp(w, Bw):
                    wv = w[:, 0:3072].rearrange("p (k n) -> p k n", n=384)
                    slot = c % GRP
                    for (c0, n) in tiles:
                        pss = [next_ps() for _ in range(3)]
                        for s in range(3):
                            mm_group(pss[s][0][:, :n], [(wv[:, k, s * 128:(s + 1) * 128], hT[:, k, c0:c0 + n]) for k in range(KC)],
                                     reads=[Bw] + hT_bufs(c0, n), writes=[pss[s][1]])
                        (psb, Bpb), (psc, Bpc), (psh, Bph) = pss
                        gcs, Bg = next_tmp()
                        P.op(act, lambda gcs=gcs, psc=psc: nc.scalar.copy(gcs[:, :n], psc[:, :n]), reads=[Bpc], writes=[Bg])
                        u, Bu = next_tmp()
                        P.op(dve, lambda u=u, psh=psh, gcs=gcs: nc.vector.tensor_tensor(u[:, :n], psh[:, :n], gcs[:, :n], ALU.mult),
                             reads=[Bph, Bg], writes=[Bu])
                        cv, Bc = next_tmp()
                        P.op(act, lambda cv=cv, u=u: nc.scalar.activation(out=cv[:, :n], in_=u[:, :n], func=AF.Copy,
                                                                             scale=convw[:, (j * 3 + 1) * KC + c:(j * 3 + 1) * KC + c + 1]),
                             reads=[Bu, B_par], writes=[Bc])
                        wdt = 256 if c0 < TCX else 64
                        u3 = u[:, :n].rearrange("p (r w) -> p r w", w=wdt)
                        c3v = cv[:, :n].rearrange("p (r w) -> p r w", w=wdt)
                        P.op(dve, lambda u3=u3, c3v=c3v: nc.vector.scalar_tensor_tensor(
                            out=c3v[:, :, 1:wdt], in0=u3[:, :, 0:wdt - 1], scalar=convw[:, (j * 3 + 0) * KC + c:(j * 3 + 0) * KC + c + 1], in1=c3v[:, :, 1:wdt],
                            op0=ALU.mult, op1=ALU.add), reads=[Bu, Bc, B_par], writes=[Bc])
                        P.op(dve, lambda u3=u3, c3v=c3v: nc.vector.scalar_tensor_tensor(
                            out=c3v[:, :, 0:wdt - 1], in0=u3[:, :, 1:wdt], scalar=convw[:, (j * 3 + 2) * KC + c:(j * 3 + 2) * KC + c + 1], in1=c3v[:, :, 0:wdt - 1],
                            op0=ALU.mult, op1=ALU.add), reads=[Bu, Bc, B_par], writes=[Bc])
                        P.op(dve, lambda psb=psb, cv=cv: nc.vector.tensor_tensor(zb[:, slot, c0:c0 + n], psb[:, :n], cv[:, :n], ALU.mult),
                             reads=[Bpb, Bc], writes=[B_z[slot]])
                return (load, comp)

            def mk_out(g0, G):
                def load(w, Bw):
                    wv = w[:, 0:G * 1024].rearrange("p (k n) -> p k n", n=1024)
                    P.dma(pool, [(wv, conv_w_out_d[j, g0 * 128:(g0 + G) * 128, :].rearrange("(k p) n -> p k n", p=128))], writes=[Bw])

                def comp(w, Bw):
                    wv = w[:, 0:G * 1024].rearrange("p (k n) -> p k n", n=1024)
                    wout_compute(wv, Bw, G, tiles, l, b, 2)
                return (load, comp)
            for c in range(0, 6):
                steps.append(mk_in(c))
            steps.append(mk_out(0, 6))
            for c in range(6, 8):
                steps.append(mk_in(c))
            steps.append(mk_out(6, 2))
            return steps

        def ffn_steps(l, b, tiles):
            steps = []

            def mk_in(f):
                def load(w, Bw):
                    wv = w[:, 0:2048].rearrange("p (k n) -> p k n", n=256)
                    P.dma(pool, [(wv[:, :, s * 128:(s + 1) * 128],
                                  ffn_w_in_d[l, :, s * DFF + f * 128:s * DFF + (f + 1) * 128].rearrange("(k p) n -> p k n", p=128))
                                 for s in range(2)], writes=[Bw])

                def comp(w, Bw):
                    wv = w[:, 0:2048].rearrange("p (k n) -> p k n", n=256)
                    slot = f % GRP
                    for (c0, n) in tiles:
                        (psg, Bpg), (psu, Bpu) = next_ps(), next_ps()
                        mm_group(psg[:, :n], [(wv[:, k, 0:128], hT[:, k, c0:c0 + n]) for k in range(KC)],
                                 reads=[Bw] + hT_bufs(c0, n), writes=[Bpg])
                        mm_group(psu[:, :n], [(wv[:, k, 128:256], hT[:, k, c0:c0 + n]) for k in range(KC)],
                                 reads=[Bw] + hT_bufs(c0, n), writes=[Bpu])
                        sg, Bs = next_tmp()
                        P.op(act, lambda sg=sg, psg=psg: nc.scalar.activation(out=sg[:, :n], in_=psg[:, :n], func=AF.Silu),
                             reads=[Bpg], writes=[Bs])
                        P.op(dve, lambda sg=sg, psu=psu: nc.vector.tensor_tensor(zb[:, slot, c0:c0 + n], psu[:, :n], sg[:, :n], ALU.mult),
                             reads=[Bpu, Bs], writes=[B_z[slot]])
                return (load, comp)

            def mk_out(g0, G):
                def load(w, Bw):
                    wv = w[:, 0:G * 1024].rearrange("p (k n) -> p k n", n=1024)
                    P.dma(pool, [(wv, ffn_w_out_d[l, g0 * 128:(g0 + G) * 128, :].rearrange("(k p) n -> p k n", p=128))], writes=[Bw])

                def comp(w, Bw):
                    wv = w[:, 0:G * 1024].rearrange("p (k n) -> p k n", n=1024)
                    wout_compute(wv, Bw, G, tiles, l, b, 5)
                return (load, comp)
            for g0 in range(0, NF, GRP):
                G = min(GRP, NF - g0)
                for f in range(g0, g0 + G):
                    steps.append(mk_in(f))
                steps.append(mk_out(g0, G))
            return steps

        LNSC = float(np.log(128.0 ** -0.5))

        def gla_phase(l, j, b, last):
            st0, Bst0 = next_stage()
            st1, Bst1 = next_stage()
            for z, (st, Bst) in enumerate(((st0, Bst0), (st1, Bst1))):
                P.dma(sp, [(st[0:16, :], gla_wa2_d[j, z, :, :]), (st[16:17, :], gla_ba_d[j, z:z + 1, :])], writes=[Bst])
                P.op(dve, lambda st=st, z=z: nc.vector.tensor_copy(wa2[0:17, z, :], st[0:17, :]), reads=[Bst], writes=[B_wa])
            P.dma(pool, [(wa1[:, :, z * 16:(z + 1) * 16], gla_wa1_d[j, z, :, :].rearrange("(k p) r -> p k r", p=128)) for z in range(2)],
                  writes=[B_wa], sembuf=B_wa)
            P.dma(sp, [(gng[:], gla_ng_d[j, :, :])], writes=[B_gl], sembuf=B_gl)
            for z in range(2):
                P.op(dve, lambda z=z: nc.vector.memset(r1aug[z], 1.0), writes=[B_r1[z]])
            for (c0, n) in ALL_TILES:
                for z in range(2):
                    ps, Bps = next_ps()
                    mm_group(ps[0:16, :n], [(wa1[:, k, z * 16:(z + 1) * 16], hT[:, k, c0:c0 + n]) for k in range(KC)],
                             reads=[B_wa] + hT_bufs(c0, n), writes=[Bps])
                    P.op(act, lambda ps=ps, z=z, c0=c0, n=n: nc.scalar.copy(r1aug[z][0:16, c0:c0 + n], ps[0:16, :n]),
                         reads=[Bps], writes=[B_r1[z]])

            def head_steps(h):
                def load_in(w, Bw):
                    wv = w[:, 0:6144].rearrange("p (k n) -> p k n", n=768)
                    srcs = [(0, 128, h * 128), (128, 128, 512 + h * 128), (256, 256, 2048 + h * 256), (512, 256, 1024 + h * 256)]
                    P.dma(pool, [(wv[:, :, o:o + wd], gla_w_in_d[j, :, s:s + wd].rearrange("(k p) n -> p k n", p=128))
                                 for (o, wd, s) in srcs], writes=[Bw])

                def load_out(w, Bw):
                    wv = w[:, 0:2048].rearrange("p (k n) -> p k n", n=1024)
                    P.dma(pool, [(wv, gla_w_out_d[j, h * 256:(h + 1) * 256, :].rearrange("(k p) n -> p k n", p=128))], writes=[Bw])

                state = {}

                def comp_in(w, Bw):
                    state["w"] = (w, Bw)

                def comp_out(wo, Bwo):
                    w, Bw = state["w"]
                    wv = w[:, 0:6144].rearrange("p (k n) -> p k n", n=768)
                    wov = wo[:, 0:2048].rearrange("p (k n) -> p k n", n=1024)
                    P.dma(sp, [(brh[:], gla_br_d[j, :, h * 256:(h + 1) * 256])], writes=[B_gl], sembuf=B_gl)
                    P.op(dve, lambda: nc.vector.memset(Sst[:], 0.0), writes=[B_S])
                    order = [1, 0] + list(range(NCH - 1, 1, -1))

                    def pre_front(idx):
                        n = order[idx]
                        cs = slice(n * 128, (n + 1) * 128)
                        pkv, Bpkv = next_ps()
                        mm_group(pkv[:, 0:128], [(hT[:, k, cs], wv[:, k, 128:256]) for k in range(KC)], reads=[Bw, B_hT[n]], writes=[Bpkv])
                        mm_group(pkv[:, 128:384], [(hT[:, k, cs], wv[:, k, 512:768]) for k in range(KC)], reads=[Bw, B_hT[n]], writes=[Bpkv])
                        psl, Bpl = next_ps()
                        mm_group(psl[:, 0:128], [(r1aug[1][0:17, cs], wa2[0:17, 1, h * 128:(h + 1) * 128])], reads=[B_r1[1], B_wa], writes=[Bpl])
                        e1, Be1 = next_tmp()
                        P.op(act, lambda: nc.scalar.activation(out=e1[:, 0:128], in_=psl[:, 0:128], func=AF.Exp, scale=-1.0),
                             reads=[Bpl], writes=[Be1])
                        P.op(act, lambda: nc.scalar.activation(out=e1[:, 0:128], in_=e1[:, 0:128], func=AF.Ln, bias=1.0),
                             reads=[Be1], writes=[Be1])
                        P.op(act, lambda: nc.scalar.copy(vtm[:, n, :], pkv[:, 128:384]), reads=[Bpkv], writes=[B_vtm])
                        return (pkv, Bpkv, e1, Be1)

                    def pre_back(idx, st):
                        n = order[idx]
                        pkv, Bpkv, e1, Be1 = st
                        P.op(act, lambda: nc.scalar.copy(sbprev[:, n, :], Sst[:]), reads=[B_S], writes=[B_sbp])
                        if idx == len(order) - 1:
                            return
                        psc, Bpc = next_ps()
                        mm_group(psc[:, 0:128], [(M4, e1[:, 0:128])], reads=[B_cst, Be1], writes=[Bpc])
                        mm_group(psc[:, 128:129], [(e1[:, 0:128], negcol)], reads=[B_cst, Be1], writes=[Bpc])
                        P.op(act, lambda: nc.scalar.activation(out=ec_t[:], in_=psc[:, 0:128], func=AF.Exp), reads=[Bpc], writes=[B_ec])
                        P.op(act, lambda: nc.scalar.activation(out=dec_t[:, 0:1], in_=psc[:, 128:129], func=AF.Exp), reads=[Bpc], writes=[B_dec])
                        kd, Bkd = kdec_t[idx % 2], B_kdec[idx % 2]
                        P.op(dve, lambda: nc.vector.tensor_tensor(kd[:], pkv[:, 0:128], ec_t[:], ALU.mult),
                             reads=[Bpkv, B_ec], writes=[Bkd])
                        psu, Bpu = next_ps()
                        mm_group(psu[:, 0:256], [(kd[:], vtm[:, n, :])], reads=[Bkd, B_vtm], writes=[Bpu])
                        P.op(dve, lambda: nc.vector.scalar_tensor_tensor(out=Sst[:], in0=Sst[:], scalar=dec_t[:, 0:1], in1=psu[:, 0:256],
                                                                         op0=ALU.mult, op1=ALU.add),
                             reads=[B_S, B_dec, Bpu], writes=[B_S])

                    st_cur = pre_front(0)
                    for idx in range(len(order)):
                        st_next = pre_front(idx + 1) if idx + 1 < len(order) else None
                        pre_back(idx, st_cur)
                        st_cur = st_next
                    P.op(dve, lambda: nc.vector.memset(Sst[:], 0.0), writes=[B_S])
                    for n in range(NCH):
                        cs = slice(n * 128, (n + 1) * 128)
                        outn = not (last and n < 2)
                        r = nb if n < 2 else b
                        sfb, Bsfb = Sbf[n % 2], B_Sbf[n % 2]
                        P.op(act, lambda sfb=sfb: nc.scalar.copy(sfb[:], Sst[:]), reads=[B_S], writes=[Bsfb])
                        pskr, Bpkr = next_ps()
                        ncol = 384 if outn else 128
                        mm_group(pskr[:, 0:ncol], [(hT[:, k, cs], wv[:, k, 128:128 + ncol]) for k in range(KC)], reads=[Bw, B_hT[n]], writes=[Bpkr])
                        if outn:
                            rr, Brr = next_tmp()
                            P.op(dve, lambda rr=rr, pskr=pskr: nc.vector.tensor_tensor(rr[:, 0:256], pskr[:, 128:384], brh[:], ALU.add),
                                 reads=[Bpkr, B_gl], writes=[Brr])
                            P.op(act, lambda rr=rr: nc.scalar.activation(out=rr[:, 256:512], in_=rr[:, 0:256], func=AF.Silu), reads=[Brr], writes=[Brr])
                        psl, Bpl = next_ps()
                        nz = 2 if outn else 1
                        for z in range(nz):
                            mm_group(psl[:, z * 128:(z + 1) * 128], [(r1aug[z][0:17, cs], wa2[0:17, z, h * 128:(h + 1) * 128])],
                                     reads=[B_r1[z], B_wa], writes=[Bpl])
                        if outn:
                            psqk, Bpqk = next_ps()
                            mm_group(psqk[:, 0:128], [(wv[:, k, 0:128], hT[:, k, cs]) for k in range(KC)], reads=[Bw, B_hT[n]], writes=[Bpqk])
                            mm_group(psqk[:, 128:256], [(wv[:, k, 128:256], hT[:, k, cs]) for k in range(KC)], reads=[Bw, B_hT[n]], writes=[Bpqk])
                        e1, Be1 = next_tmp()
                        P.op(act, lambda e1=e1, psl=psl, nz=nz: nc.scalar.activation(out=e1[:, 0:128 * nz], in_=psl[:, 0:128 * nz], func=AF.Exp, scale=-1.0),
                             reads=[Bpl], writes=[Be1])
                        P.op(act, lambda e1=e1, nz=nz: nc.scalar.activation(out=e1[:, 0:128 * nz], in_=e1[:, 0:128 * nz], func=AF.Ln, bias=1.0),
                             reads=[Be1], writes=[Be1])
                        psc, Bpc = next_ps()
                        mm_group(psc[:, 0:128], [(M2, e1[:, 0:128])], reads=[B_cst, Be1], writes=[Bpc])
                        mm_group(psc[:, 128:256], [(e1[:, 0:128], M1)], reads=[B_cst, Be1], writes=[Bpc])
                        if outn:
                            mm_group(psc[:, 256:384], [(e1[:, 128:256], M3)], reads=[B_cst, Be1], writes=[Bpc])
                        P.op(act, lambda psc=psc: nc.scalar.activation(out=ec_t[:], in_=psc[:, 0:128], func=AF.Exp), reads=[Bpc], writes=[B_ec])
                        P.op(act, lambda psc=psc: nc.scalar.activation(out=dec_t[:, 1:2], in_=psc[:, 255:256], func=AF.Exp), reads=[Bpc], writes=[B_dec])
                        kd, Bkd = kdec_t[n % 2], B_kdec[n % 2]
                        P.op(dve, lambda kd=kd, pskr=pskr: nc.vector.tensor_tensor(kd[:], pskr[:, 0:128], ec_t[:], ALU.mult),
                             reads=[Bpkr, B_ec], writes=[Bkd])
                        if outn:
                            P.op(act, lambda psc=psc: nc.scalar.activation(out=Eq_t[:], in_=psc[:, 128:384], func=AF.Exp, bias=LNSC),
                                 reads=[Bpc], writes=[B_Eq])
                            P.op(act, lambda psc=psc: nc.scalar.activation(out=Ek_t[:], in_=psc[:, 128:384], func=AF.Exp, scale=-1.0),
                                 reads=[Bpc], writes=[B_Ek])
                            for z in range(2):
                                P.op(dve, lambda z=z, psqk=psqk: nc.vector.tensor_tensor(qdec_t[:, z, :], psqk[:, 0:128], Eq_t[:, z * 128:(z + 1) * 128], ALU.mult),
                                     reads=[Bpqk, B_Eq], writes=[B_qdec])
                                P.op(dve, lambda z=z, psqk=psqk: nc.vector.tensor_tensor(kinv_t[:, z, :], psqk[:, 128:256], Ek_t[:, z * 128:(z + 1) * 128], ALU.mult),
                                     reads=[Bpqk, B_Ek], writes=[B_kinv])
                            pss, Bpss = next_ps()
                            for z in range(2):
                                mm_group(pss[:, z * 128:(z + 1) * 128], [(kinv_t[:, z, :], qdec_t[:, z, :])], reads=[B_kinv, B_qdec], writes=[Bpss])
                            P.op(dve, lambda pss=pss: nc.vector.tensor_tensor(sm_t[:].rearrange("p (z c) -> p z c", z=2),
                                                                              pss[:, 0:256].rearrange("p (z c) -> p z c", z=2), mask2, ALU.mult),
                                 reads=[Bpss, B_cst], writes=[B_sm])
                            pso, Bpo = next_ps()
                            mm_group(pso[:, 0:256], [(sm_t[:, 0:128], vtm[:, n, :]), (qdec_t[:, 0, :], sfb[:]),
                                                     (sm_t[:, 128:256], vtm[:, n, :]), (qdec_t[:, 1, :], sbprev[:, n, :])],
                                     reads=[B_sm, B_vtm, B_qdec, Bsfb, B_sbp], writes=[Bpo])
                        if n < NCH - 1:
                            psu, Bpu = next_ps()
                            mm_group(psu[:, 0:256], [(kd[:], vtm[:, n, :])], reads=[Bkd, B_vtm], writes=[Bpu])
                            P.op(dve, lambda psu=psu: nc.vector.scalar_tensor_tensor(out=Sst[:], in0=Sst[:], scalar=dec_t[:, 1:2], in1=psu[:, 0:256],
                                                                                    op0=ALU.mult, op1=ALU.add),
                                 reads=[B_S, B_dec, Bpu], writes=[B_S])
                        if not outn:
                            continue
                        P.op(act, lambda pso=pso: nc.scalar.activation(out=og_t[:], in_=pso[:, 0:256], func=AF.Square, accum_out=dec_t[:, 2:3]),
                             reads=[Bpo], writes=[B_og, B_dec])
                        P.op(act, lambda: nc.scalar.activation(out=dec_t[:, 3:4], in_=dec_t[:, 2:3], func=AF.Ln, bias=EPS, scale=1.0 / 256),
                             reads=[B_dec], writes=[B_dec])
                        P.op(act, lambda: nc.scalar.activation(out=dec_t[:, 3:4], in_=dec_t[:, 3:4], func=AF.Exp, scale=-0.5),
                             reads=[B_dec], writes=[B_dec])
                        t1, Bt1 = next_tmp()
                        P.op(dve, lambda t1=t1, pso=pso: nc.vector.scalar_tensor_tensor(out=t1[:, 0:256], in0=pso[:, 0:256], scalar=dec_t[:, 3:4], in1=gng[:],
                                                                                     op0=ALU.mult, op1=ALU.mult),
                             reads=[Bpo, B_dec, B_gl], writes=[Bt1])
                        P.op(dve, lambda t1=t1, rr=rr: nc.vector.tensor_tensor(og_t[:], t1[:, 0:256], rr[:, 256:512], ALU.mult),
                             reads=[Bt1, Brr], writes=[B_og])

                        def tr():
                            nc.tensor.transpose(ptb[:, 0:128], og_t[:, 0:128], ident_b[:])
                            return nc.tensor.transpose(ptb[:, 128:256], og_t[:, 128:256], ident_b[:])
                        P.op(pe, tr, reads=[B_og, B_idb], writes=[B_ptb])
                        P.op(act, lambda: nc.scalar.copy(ogT_t[:].rearrange("p j t -> p (j t)"), ptb[:, 0:256]), reads=[B_ptb], writes=[B_ogT])
                        for half in range(2):
                            psy, Bpy = next_ps()
                            for q in range(4):
                                cp = half * 4 + q
                                mm_group(psy[:, q * 128:(q + 1) * 128], [(wov[:, jj, cp * 128:(cp + 1) * 128], ogT_t[:, jj, :]) for jj in range(2)],
                                         reads=[Bwo, B_ogT], writes=[Bpy])
                            for q in range(4):
                                cp = half * 4 + q
                                P.op(dve, lambda psy=psy, q=q, cp=cp, r=r, cs=cs: nc.vector.scalar_tensor_tensor(
                                    out=xs[:, cp, cs], in0=psy[:, q * 128:(q + 1) * 128], scalar=ada_ap(l, r, 2, cp), in1=xs[:, cp, cs],
                                    op0=ALU.mult, op1=ALU.add), reads=[Bpy, B_ada, B_xs[cp][n]], writes=[B_xs[cp][n]])
                return [(load_in, comp_in), (load_out, comp_out)]
            steps = []
            for h in range(4):
                steps += head_steps(h)
            return steps

        def final_phase(b):
            for (c0, n) in LAT_TILES:
                rstd_tile(c0, n)
                for k in range(KC):
                    t, Bt = next_tmp()
                    P.op(dve, lambda t=t, k=k: nc.vector.tensor_tensor(t[:, :n], xs[:, k, c0:c0 + n], rs[:, :n], ALU.mult),
                         reads=xs_bufs(k, c0, n) + [B_rs], writes=[Bt])
                    P.op(act, lambda t=t, k=k: nc.scalar.activation(out=xs[:, k, c0:c0 + n], in_=t[:, :n], func=AF.Copy, scale=fing[:, k:k + 1]),
                         reads=[Bt, B_par], writes=xs_bufs(k, c0, n))

        if "ada" in phases:
            run_steps(ada_steps())
        for b in range(nb):
            load_x(b)
            for l in layers:
                last = (l == DEPTH - 1)
                kind = l % 2
                j = l // 2
                if "norm1" in phases:
                    norm_phase(l, 0, b, ALL_TILES)
                if "mixer" in phases:
                    if kind == 0:
                        run_steps(conv_steps(l, j, b))
                    else:
                        run_steps(gla_phase(l, j, b, last), dist=1)
                tiles = LAT_TILES if last else ALL_TILES
                if "norm2" in phases:
                    norm_phase(l, 1, b, tiles)
                if "ffn" in phases:
                    run_steps(ffn_steps(l, b, tiles))
            if final:
                final_phase(b)
            store_x(b, range(2, NCH), xo_d, 2)
            if want_ctx_out:
                store_x(b, range(0, 2), ctxo_d, 0)
        for Bst in B_stage:
            if Bst.dsem is not None:
                sp.e.wait_ge(Bst.dsem[0], Bst.dsem[1])
    return nc


def _consts():
    s = np.arange(128)[:, None]
    t = np.arange(128)[None, :]
    c = np.zeros((128, 8, 128), np.float32)
    c[:, 0, :] = np.eye(128, dtype=np.float32)
    c[:, 1, :] = np.where(s <= t, -1.0 / 16, 0.0)
    c[:, 2, :] = np.where(s > t, -1.0 / 16, 0.0)
    c[:, 3, :] = np.where(s >= t, -1.0 / 16, 0.0)
    c[:, 4, :] = np.where(s < t, -1.0 / 16, 0.0)
    c[:, 5, :] = np.where(s <= t, 1.0, 0.0)
    c[:, 6, :] = np.where(s >= t, 1.0, 0.0)
    c[:, 7, :] = -1.0 / 16
    return c


def _fm(v):
    v = np.asarray(v, np.float32)
    lead = v.shape[:-1]
    a = v.reshape(lead + (KC, 128))
    a = np.moveaxis(a, -1, 0)
    return np.ascontiguousarray(a)


def make_in_maps(x, c, ctx, c_ctx, ada_w, ada_b, norm1_g, norm2_g, conv_w_in, conv_w, conv_w_out,
                 gla_w_in, gla_b_r, gla_w_a1, gla_w_a2, gla_b_a, gla_norm_g, gla_w_out,
                 ffn_w_in, ffn_w_out, final_g, ncores=NCORES):
    f = lambda a: np.ascontiguousarray(np.asarray(a, np.float32))
    shared = {
        "ada_w": f(ada_w),
        "ada_b": np.ascontiguousarray(np.moveaxis(f(ada_b).reshape(DEPTH, 48, 128), -1, 0)),
        "n1g": _fm(norm1_g), "n2g": _fm(norm2_g),
        "conv_w_in": f(conv_w_in), "convw": _fm(conv_w), "conv_w_out": f(conv_w_out),
        "gla_w_in": f(gla_w_in),
        "gla_br": np.ascontiguousarray(np.broadcast_to(f(gla_b_r)[:, None, :], (2, 128, D))),
        "gla_wa1": f(gla_w_a1), "gla_wa2": f(gla_w_a2), "gla_ba": f(gla_b_a),
        "gla_ng": np.ascontiguousarray(np.broadcast_to(f(gla_norm_g)[:, None, :], (2, 128, 256))),
        "gla_w_out": f(gla_w_out), "ffn_w_in": f(ffn_w_in), "ffn_w_out": f(ffn_w_out),
        "fing": _fm(final_g), "cst": _consts(),
    }
    x = f(x)
    c = f(c)
    ctx = f(ctx)
    c_ctx = f(c_ctx)
    maps = []
    nb = 16 // ncores
    for i in range(ncores):
        c3 = np.concatenate([c[nb * i:nb * (i + 1)], c_ctx[None, :]], 0)
        m = dict(shared)
        m["x"] = np.ascontiguousarray(x[nb * i:nb * (i + 1)])
        m["ctx"] = np.ascontiguousarray(ctx[nb * i:nb * (i + 1)])
        m["c3"] = np.ascontiguousarray(np.moveaxis(c3.reshape(nb + 1, KC, 128), (0, 1, 2), (2, 1, 0)))
        maps.append(m)
    return maps


_NC_CACHE = {}


def kernel(**inputs):
    ncores = CORES_PER_LAUNCH
    maps = make_in_maps(ncores=ncores, **inputs)
    if "full" not in _NC_CACHE:
        _NC_CACHE["full"] = build_program(nb=16 // ncores)
    nc = _NC_CACHE["full"]
    res = run_bass_kernel_spmd(nc, maps, core_ids=list(range(ncores)))
    return np.concatenate([np.asarray(r["xo"], np.float32) for r in res.results], axis=0)
```
